# Optimizing a Trainium2 kernel written in Bass

```python
import math
import jax, jax.numpy as jnp
from jax import lax
import numpy as np

D_MODEL = 4096
BATCH = 4
SEQ = 2048
DEPTH = 4

D_MIX = D_MODEL
ATT_WIDTH = D_MIX // 2
CONV_WIDTH = D_MIX - ATT_WIDTH

N_HEADS = 16
V_HEAD_DIM = ATT_WIDTH // N_HEADS
QK_NOPE_DIM = 128
QK_ROPE_DIM = 64
QK_HEAD_DIM = QK_NOPE_DIM + QK_ROPE_DIM
Q_LORA_RANK = 1024
KV_LORA_RANK = 512
ROPE_THETA = 10000.0
Q_BLOCK = 128

CONV_GROUPS = 16
CONV_KERNEL = 31

EPS = 1e-6

IN_SIZES = (Q_LORA_RANK, KV_LORA_RANK, QK_ROPE_DIM, ATT_WIDTH, 2 * CONV_WIDTH, CONV_WIDTH)
IN_COLS = sum(IN_SIZES)
IN_SPLITS = tuple(int(s) for s in np.cumsum(IN_SIZES)[:-1])

kernel_name = "hymba_mla_conformer_hybrid"


def rmsnorm(x, g):
    xf = x.astype(jnp.float32)
    y = xf * lax.rsqrt(jnp.mean(xf * xf, axis=-1, keepdims=True) + EPS)
    return (y * g.astype(jnp.float32)).astype(x.dtype)


def layernorm(x, g, b):
    xf = x.astype(jnp.float32)
    mu = jnp.mean(xf, axis=-1, keepdims=True)
    xc = xf - mu
    var = jnp.mean(xc * xc, axis=-1, keepdims=True)
    y = xc * lax.rsqrt(var + EPS)
    return (y * g.astype(jnp.float32) + b.astype(jnp.float32)).astype(x.dtype)


def rope_tables(positions, dtype):
    half = QK_ROPE_DIM // 2
    inv_freq = ROPE_THETA ** (-jnp.arange(half, dtype=jnp.float32) / half)
    ang = positions.astype(jnp.float32)[..., None] * inv_freq
    return jnp.cos(ang)[:, :, None, :].astype(dtype), jnp.sin(ang)[:, :, None, :].astype(dtype)


def apply_rope(x, cos, sin):
    x1, x2 = jnp.split(x, 2, axis=-1)
    return jnp.concatenate([x1 * cos - x2 * sin, x2 * cos + x1 * sin], axis=-1)


def causal_attention(q, k, v):
    B, S, H, Dq = q.shape
    Dv = v.shape[-1]
    nb = S // Q_BLOCK
    scale = 1.0 / math.sqrt(Dq)
    qb = q.reshape(B, nb, Q_BLOCK, H, Dq).transpose(1, 0, 2, 3, 4)
    kpos = jnp.arange(S)

    def one_block(args):
        i, qi = args
        s = jnp.einsum('bqhd,bkhd->bhqk', qi, k, preferred_element_type=jnp.float32) * scale
        qpos = i * Q_BLOCK + jnp.arange(Q_BLOCK)
        mask = kpos[None, :] <= qpos[:, None]
        s = jnp.where(mask[None, None], s, -jnp.inf)
        p = jax.nn.softmax(s, axis=-1)
        return jnp.einsum('bhqk,bkhd->bqhd', p.astype(v.dtype), v)

    out = lax.map(one_block, (jnp.arange(nb), qb))
    return out.transpose(1, 0, 2, 3, 4).reshape(B, S, H, Dv)


def causal_depthwise_conv(u, w, b):
    C = u.shape[-1]
    y = lax.conv_general_dilated(
        u, w[:, None, :].astype(u.dtype),
        window_strides=(1,), padding=((CONV_KERNEL - 1, 0),),
        dimension_numbers=('NWC', 'WIO', 'NWC'), feature_group_count=C)
    return y + b


def setup_inputs(seed: int = 0) -> dict:
    key = jax.random.key(seed)
    ks = jax.random.split(key, 16)
    f32 = jnp.float32

    def nrm(k, shape, scale):
        return jax.random.normal(k, shape, f32) * scale

    def gain(k, shape):
        return 1.0 + 0.01 * jax.random.normal(k, shape, f32)

    x = jax.random.normal(ks[0], (BATCH, SEQ, D_MODEL), f32)
    offset = jax.random.randint(ks[1], (BATCH, 1), 0, 4096, dtype=jnp.int32)
    positions = (jnp.arange(SEQ, dtype=jnp.int32)[None, :] + offset).astype(jnp.int32)
    return {
        "x": x,
        "positions": positions,
        "ln_g": gain(ks[2], (DEPTH, D_MODEL)),
        "w_in": nrm(ks[3], (DEPTH, D_MODEL, IN_COLS), D_MODEL ** -0.5),
        "q_a_norm": gain(ks[4], (DEPTH, Q_LORA_RANK)),
        "w_q_up": nrm(ks[5], (DEPTH, Q_LORA_RANK, N_HEADS * QK_HEAD_DIM), Q_LORA_RANK ** -0.5),
        "kv_a_norm": gain(ks[6], (DEPTH, KV_LORA_RANK)),
        "w_kv_up": nrm(ks[7], (DEPTH, KV_LORA_RANK, N_HEADS * (QK_NOPE_DIM + V_HEAD_DIM)), KV_LORA_RANK ** -0.5),
        "q_norm": gain(ks[8], (DEPTH, QK_HEAD_DIM)),
        "k_norm": gain(ks[9], (DEPTH, QK_HEAD_DIM)),
        "w_dw": nrm(ks[10], (DEPTH, CONV_KERNEL, CONV_WIDTH), CONV_KERNEL ** -0.5),
        "b_dw": nrm(ks[11], (DEPTH, CONV_WIDTH), 0.01),
        "conv_ln_g": gain(ks[12], (DEPTH, CONV_WIDTH)),
        "conv_ln_b": nrm(ks[13], (DEPTH, CONV_WIDTH), 0.01),
        "w_out": nrm(ks[14], (DEPTH, D_MIX, D_MODEL), D_MIX ** -0.5),
    }


def reference(x, positions, ln_g, w_in, q_a_norm, w_q_up, kv_a_norm, w_kv_up,
              q_norm, k_norm, w_dw, b_dw, conv_ln_g, conv_ln_b, w_out):
    B, S, _ = x.shape
    cos, sin = rope_tables(positions, x.dtype)
    for l in range(DEPTH):
        h = rmsnorm(x, ln_g[l])
        z = h @ w_in[l]
        q_c, kv_c, k_pe, g_att, u_conv, g_conv = jnp.split(z, IN_SPLITS, axis=-1)

        q = (rmsnorm(q_c, q_a_norm[l]) @ w_q_up[l]).reshape(B, S, N_HEADS, QK_HEAD_DIM)
        kv = (rmsnorm(kv_c, kv_a_norm[l]) @ w_kv_up[l]).reshape(B, S, N_HEADS, QK_NOPE_DIM + V_HEAD_DIM)
        k_nope, v = kv[..., :QK_NOPE_DIM], kv[..., QK_NOPE_DIM:]
        k_pe_h = jnp.broadcast_to(k_pe[:, :, None, :], (B, S, N_HEADS, QK_ROPE_DIM))
        k = jnp.concatenate([k_nope, k_pe_h], axis=-1)
        q = rmsnorm(q, q_norm[l])
        k = rmsnorm(k, k_norm[l])
        q = jnp.concatenate([q[..., :QK_NOPE_DIM], apply_rope(q[..., QK_NOPE_DIM:], cos, sin)], axis=-1)
        k = jnp.concatenate([k[..., :QK_NOPE_DIM], apply_rope(k[..., QK_NOPE_DIM:], cos, sin)], axis=-1)
        att = causal_attention(q, k, v).reshape(B, S, ATT_WIDTH)
        att = att * jax.nn.silu(g_att)

        a, b = jnp.split(u_conv, 2, axis=-1)
        u = a * jax.nn.sigmoid(b)
        c = causal_depthwise_conv(u, w_dw[l], b_dw[l])
        c = jax.nn.silu(layernorm(c, conv_ln_g[l], conv_ln_b[l]))
        c = c * jax.nn.silu(g_conv)

        y = jnp.concatenate([att, c], axis=-1) @ w_out[l]
        x = x + y
    return x
```

```python
from contextlib import ExitStack
import math
import numpy as np
import concourse.bass as bass
import concourse.mybir as mybir
from concourse.bass_utils import run_bass_kernel_spmd

F32 = mybir.dt.float32
BF16 = mybir.dt.bfloat16
I32 = mybir.dt.int32
ALU = mybir.AluOpType
AF = mybir.ActivationFunctionType

D = 4096
SEQ = 2048
DEPTH = 4
TOK = 1024
TH = 512
NPL = TOK // TH
NH = 16
EPS = 1e-6
NB_IN = 77
NB_A1 = 13
NSP = 592
SAME_ENGINE_SYNC = True
N_DMA_SEMS = 24
N_CORES = 8
HALO = 32


class Buf:
    __slots__ = ("name", "w", "rs")

    def __init__(self, name):
        self.name = name
        self.w = None
        self.rs = []


class Sched:
    def __init__(self, nc, es):
        self.nc = nc
        self.eng = {"pe": nc.tensor, "act": nc.scalar, "dve": nc.vector,
                    "pool": nc.gpsimd, "sp": nc.sync}
        self.sem = {}
        self.cnt = {}
        for k in self.eng:
            self.sem[k] = es.enter_context(nc.semaphore("s_" + k))
            self.cnt[k] = 0
        self.dsem = [es.enter_context(nc.semaphore("d%d" % i)) for i in range(N_DMA_SEMS)]
        self.dcnt = [0] * N_DMA_SEMS
        self.dnext = 0
        self.csem = es.enter_context(nc.semaphore("s_coll"))
        self.ccnt = 0
        self.seen = {k: {} for k in self.eng}
        self.n_inst = 0
        self.nosync = False

    def _semof(self, key):
        if isinstance(key, tuple):
            return self.dsem[key[1]]
        if key == "coll":
            return self.csem
        return self.sem[key]

    def _wait(self, E, tok):
        key, val = tok
        if key == E and (E == "pe" or not SAME_ENGINE_SYNC or self.nosync):
            return
        if self.seen[E].get(key, 0) >= val:
            return
        self.seen[E][key] = val
        self.eng[E].wait_ge(self._semof(key), val)
        self.n_inst += 1

    def _deps(self, E, reads, writes):
        for b in reads:
            if b.w is not None:
                self._wait(E, b.w)
        for b in writes:
            if b.w is not None:
                self._wait(E, b.w)
            for t in b.rs:
                self._wait(E, t)

    def _record(self, tok, reads, writes):
        for b in reads:
            b.rs.append(tok)
        for b in writes:
            b.w = tok
            b.rs = []

    def op(self, E, fn, reads=(), writes=(), inc=True, nosync=False):
        self.nosync = nosync
        self._deps(E, reads, writes)
        self.nosync = False
        inst = fn(self.eng[E])
        tok = (E, self.cnt[E] + 1)
        if inc:
            inst.then_inc(self.sem[E], 1)
            self.cnt[E] += 1
        self._record(tok, reads, writes)
        self.n_inst += 1
        return tok

    def dma(self, Q, out, in_, reads=(), writes=()):
        self._deps(Q, reads, writes)
        i = self.dnext
        self.dnext = (self.dnext + 1) % N_DMA_SEMS
        key = ("d", i)
        if self.dcnt[i] > 0:
            self._wait(Q, (key, self.dcnt[i]))
        inst = self.eng[Q].dma_start(out=out, in_=in_)
        self.dcnt[i] += 16
        inst.then_inc(self.dsem[i], 16)
        tok = (key, self.dcnt[i])
        self._record(tok, reads, writes)
        self.n_inst += 1
        return tok

    def coll(self, in_ap, out_ap, groups, reads, writes):
        self._deps("pool", reads, writes)
        inst = self.nc.gpsimd.collective_compute("AllGather", ALU.bypass, replica_groups=groups,
                                                 ins=[in_ap], outs=[out_ap])
        inst.then_inc(self.csem)
        self.ccnt += 1
        tok = ("coll", self.ccnt)
        self._record(tok, reads, writes)
        self.n_inst += 1
        return tok

    def wait_bufs(self, E, bufs):
        for b in bufs:
            if b.w is not None:
                self._wait(E, b.w)
            for t in b.rs:
                self._wait(E, t)


def build_program(NL, n_cores=N_CORES):
    nc = bass.Bass("TRN2", target_bir_lowering=False)
    dt_in = lambda name, shape, dt=F32: nc.dram_tensor(name, shape, dt, kind="ExternalInput").ap()
    xT = dt_in("xT", [D, TOK])
    pos = dt_in("pos", [1, TOK], I32)
    w_in_r = dt_in("w_in_r", [NL * NB_IN * 128, 4096])
    w_out_r = dt_in("w_out_r", [NL * 32 * 128, 4096])
    wq_r = dt_in("wq_r", [NL * NH * 128, 8 * 192])
    wk_r = dt_in("wk_r", [NL * NH * 128, 4 * 128])
    wv_r = dt_in("wv_r", [NL * 4 * 128, 4 * 512])
    sp_r = dt_in("sp_r", [NL * 128, NSP])
    cst = dt_in("cst", [128, 130])
    flg = dt_in("flg", [128, 2])
    out = nc.dram_tensor("out", [D, TOK], F32, kind="ExternalOutput").ap()
    kc_a_t = [nc.dram_tensor("kc_a%d" % i, [8 * 128, TOK], BF16) for i in range(2)]
    kc_b_t = nc.dram_tensor("kc_b", [NH * 64, TOK], BF16)
    vcp_t = [nc.dram_tensor("vcp%d" % i, [4 * TOK, 256], BF16) for i in range(2)]
    xh_t = nc.dram_tensor("xh", [D, HALO], F32)
    kc_a_gt = [nc.dram_tensor("kc_a_g%d" % i, [2 * 8 * 128, TOK], BF16) for i in range(2)]
    kc_b_gt = nc.dram_tensor("kc_b_g", [2 * NH * 64, TOK], BF16)
    vcp_gt = [nc.dram_tensor("vcp_g%d" % i, [2 * 4 * TOK, 256], BF16) for i in range(2)]
    xh_gt = nc.dram_tensor("xh_g", [2 * D, HALO], F32)
    kc_b, xh = kc_b_t.ap(), xh_t.ap()
    kc_b_g, xh_g = kc_b_gt.ap(), xh_gt.ap()

    def kc_a_rows(h):
        return kc_a_t[h // 8].ap()[(h % 8) * 128:(h % 8 + 1) * 128, :]

    def kc_a_g_rows(h):
        return kc_a_gt[h // 8].ap()[(h % 8) * 128:(h % 8 + 1) * 128, :]

    def vcp_rows(pr, r0, r1):
        return vcp_t[pr // 4].ap()[(pr % 4) * TOK + r0:(pr % 4) * TOK + r1, :]

    def vcp_g_rows(pr):
        return vcp_gt[pr // 4].ap()[(pr % 4) * TOK:(pr % 4 + 1) * TOK, :]
    hs = nc.dram_tensor("hs", [NPL * D, TH], BF16).ap()
    qs = nc.dram_tensor("qs", [NPL * 1024, TH], BF16).ap()
    groups = [[2 * i, 2 * i + 1] for i in range(n_cores // 2)]

    with ExitStack() as es:
        S = Sched(nc, es)

        def sb(name, shape, dt):
            return es.enter_context(nc.sbuf_tensor(name, shape, dt))

        hT = sb("hT", [128, 32, TH], BF16)
        catT = sb("catT", [128, 32, TH], BF16)
        NW = 4
        wbuf = [sb("wbuf%d" % i, [128, 4, 1024], BF16) for i in range(NW)]
        wq = [sb("wq%d" % i, [128, 8, 192], BF16) for i in range(2)]
        wk = [sb("wk%d" % i, [128, 4, 128], BF16) for i in range(2)]
        wv = sb("wv", [128, 4, 512], BF16)
        qcn = sb("qcn", [128, 8, TH], BF16)
        kvn = sb("kvn", [128, 4, TH], BF16)
        stage = sb("stage", [128, 8, TH], F32)
        sqt = [sb("sqt%d" % i, [128, TH], BF16) for i in range(2)]
        rbc = sb("rbc", [128, TH], F32)
        rbc2 = sb("rbc2", [128, TH], F32)
        spe = sb("spe", [128, TH], F32)
        kpe_f = sb("kpe_f", [64, TH], F32)
        kpe_rot = sb("kpe_rot", [64, TH], F32)
        ropeT = sb("ropeT", [64, TH], F32)
        ropeU = sb("ropeU", [64, TH], F32)
        qb_f = sb("qb_f", [64, TH], F32)
        cos64 = sb("cos64", [64, TOK], F32)
        sw64 = sb("sw64", [64, TOK], F32)
        par = sb("par", [128, NSP], F32)
        cst_t = sb("cst_t", [128, 2], F32)
        flg_t = sb("flg_t", [128, 2], F32)
        tri = sb("tri", [128, 128], BF16)
        ones = sb("ones", [128, 128], BF16)
        halo = sb("halo", [128, 16, 30], F32)
        hTh = sb("hTh", [128, 32, HALO], BF16)
        rh = sb("rh", [128, HALO], F32)
        sgh = sb("sgh", [128, HALO], F32)
        uh = sb("uh", [128, HALO], F32)
        ubuf = [sb("ubuf%d" % i, [128, 30 + TH], F32) for i in range(2)]
        sig = [sb("sig%d" % i, [128, TH], F32) for i in range(2)]
        acc = [sb("acc%d" % i, [128, TH], F32) for i in range(2)]
        tmpf = [sb("tmpf%d" % i, [128, TH], F32) for i in range(2)]
        mu_bc = sb("mu_bc", [128, TH], F32)
        rs_bc = sb("rs_bc", [128, TH], F32)
        qt_a = [sb("qt_a%d" % i, [128, TH], BF16) for i in range(2)]
        qt_b = [sb("qt_b%d" % i, [64, TH], BF16) for i in range(2)]
        pT = [sb("pT%d" % i, [128, TH], BF16) for i in range(3)]
        ps = [es.enter_context(nc.psum_tensor("ps%d" % i, [128, TH], F32)) for i in range(8)]

        B_hT = [Buf("hT%d" % c) for c in range(32)]
        B_cat = [Buf("cat%d" % c) for c in range(32)]
        B_wbuf = [Buf("wbuf%d" % i) for i in range(NW)]
        B_wq = [Buf("wq%d" % i) for i in range(2)]
        B_wk = [Buf("wk%d" % i) for i in range(2)]
        B_wv = Buf("wv")
        B_qcn = [Buf("qcn%d" % c) for c in range(8)]
        B_kvn = [Buf("kvn%d" % c) for c in range(4)]
        B_stage = [Buf("stage%d" % c) for c in range(8)]
        B_sqt = [Buf("sqt%d" % i) for i in range(2)]
        B = {k: Buf(k) for k in ["rbc", "rbc2", "spe", "kpe_f", "kpe_rot", "ropeT", "ropeU", "qb_f",
                                 "tab", "par", "cst", "mu", "rs", "hTh", "rh", "sgh", "uh",
                                 "xh", "xh_g", "kc_g", "vc_g"]}
        B_halo = [Buf("halo%d" % j) for j in range(16)]
        B_ubuf = [Buf("ubuf%d" % i) for i in range(2)]
        B_sig = [Buf("sig%d" % i) for i in range(2)]
        B_acc = [Buf("acc%d" % i) for i in range(2)]
        B_tmpf = [Buf("tmpf%d" % i) for i in range(2)]
        B_qta = [Buf("qta%d" % i) for i in range(2)]
        B_qtb = [Buf("qtb%d" % i) for i in range(2)]
        B_pT = [Buf("pT%d" % i) for i in range(3)]
        B_ps = [Buf("ps%d" % i) for i in range(8)]
        B_xin = B_stage[0:3]
        xin = [stage[:, i, :] for i in range(3)]
        B_x = [[Buf("x_%d_%d" % (p, j)) for j in range(32)] for p in range(NPL)]
        B_kc = [Buf("kc%d" % h) for h in range(NH)]
        B_vc = [Buf("vc%d" % g) for g in range(8)]
        B_hs = [Buf("hs%d" % p) for p in range(NPL)]
        B_qs = [Buf("qs%d" % p) for p in range(NPL)]

        def flat(ap):
            return ap.rearrange("p a t -> p (a t)")
        vgw = [hT[:, 8 * i:8 * i + 8, :].rearrange("p a (k d) -> p (a k) d", d=256) for i in range(2)]
        Bv_vgw = [B_hT[8 * i:8 * i + 8] for i in range(2)]
        ktw_a = [flat(hT[:, 16 + 8 * i:20 + 8 * i, :]) for i in range(2)]
        ktw_b = [flat(hT[0:64, 20 + 8 * i:24 + 8 * i, :]) for i in range(2)]
        Bv_kta = [B_hT[16 + 8 * i:20 + 8 * i] for i in range(2)]
        Bv_ktb = [B_hT[20 + 8 * i:24 + 8 * i] for i in range(2)]
        vtmp = hT[:, 0:4, :]
        Bv_vtmp = B_hT[0:4]
        ktmp_a = [hT[:, 4 + i, :] for i in range(4)]
        ktmp_b = [hT[0:64, 8 + i, :] for i in range(4)]

        def mm(o, l, r, start, stop, reads, writes, inc=True):
            S.op("pe", lambda e: e.matmul(o, l, r, start=start, stop=stop), reads, writes, inc)

        def act(o, i, func, reads, writes, bias=0.0, scale=1.0):
            S.op("act", lambda e: e.activation(out=o, in_=i, func=func, bias=bias, scale=scale), reads, writes)

        def acopy(o, i, reads, writes):
            S.op("act", lambda e: e.copy(out=o, in_=i), reads, writes)

        def tt(o, a, b, op, reads, writes, E="dve"):
            S.op(E, lambda e: e.tensor_tensor(out=o, in0=a, in1=b, op=op), reads, writes)

        def stt(o, a, s, b, op0, op1, reads, writes, E="dve", nosync=False):
            S.op(E, lambda e: e.scalar_tensor_tensor(out=o, in0=a, scalar=s, in1=b, op0=op0, op1=op1), reads, writes,
                 nosync=nosync)

        def ts(o, a, s1, s2, op0, op1, reads, writes, E="dve"):
            if s2 is None:
                S.op(E, lambda e: e.tensor_scalar(out=o, in0=a, scalar1=s1, scalar2=None, op0=op0), reads, writes)
            else:
                S.op(E, lambda e: e.tensor_scalar(out=o, in0=a, scalar1=s1, scalar2=s2, op0=op0, op1=op1), reads, writes)

        def cp(o, i, reads, writes, E="dve"):
            S.op(E, lambda e: e.tensor_copy(out=o, in_=i), reads, writes)

        def recip(o, i, reads, writes):
            S.op("dve", lambda e: e.reciprocal(out=o, in_=i), reads, writes)

        def rstd_from(dst, dstB, src, srcBs, n):
            act(dst, src, AF.Ln, srcBs, [dstB], bias=EPS, scale=1.0 / n)
            act(dst, dst, AF.Exp, [dstB], [dstB], scale=-0.5)

        S.dma("sp", cst_t[:], cst[:, 0:2], writes=[B["cst"]])
        S.dma("sp", flg_t[:], flg[:, :], writes=[B["cst"]])
        S.dma("pool", tri[:], cst[:, 2:130], writes=[B["cst"]])
        S.op("dve", lambda e: e.memset(ones[:], 1.0), writes=[B["cst"]])
        PI = float(np.pi)
        C1 = 6.28125
        C2 = 2 * PI - C1
        posi = stage[0:64, 0:2, :].bitcast(I32)
        posf = stage[0:64, 2:4, :]
        ang_a = stage[0:64, 4:6, :]
        ang_k = stage[0:64, 6:8, :]
        SB_ = B_stage
        S.dma("sp", posi.rearrange("p a t -> p (a t)"), pos.partition_broadcast(64), writes=SB_)
        cp(posf, posi, SB_, SB_)

        def reduce_sin(dst, shift):
            ts(ang_a, posf, cst_t[0:64, 0:1], shift, ALU.mult, ALU.add, SB_ + [B["cst"]], SB_)
            ts(ang_k, ang_a, 1.0 / (2 * PI), None, ALU.mult, None, SB_, SB_)
            cp(posi, ang_k, SB_, SB_)
            cp(ang_k, posi, SB_, SB_)
            stt(ang_a, ang_k, -C1, ang_a, ALU.mult, ALU.add, SB_, SB_)
            stt(ang_a, ang_k, -C2, ang_a, ALU.mult, ALU.add, SB_, SB_)
            ts(ang_k, ang_a, PI, -2 * PI, ALU.is_gt, ALU.mult, SB_, SB_)
            tt(ang_a, ang_a, ang_k, ALU.add, SB_, SB_)
            ts(ang_k, ang_a, -PI, 2 * PI, ALU.is_lt, ALU.mult, SB_, SB_)
            tt(ang_a, ang_a, ang_k, ALU.add, SB_, SB_)
            act(dst, ang_a, AF.Sin, SB_, [B["tab"]])

        reduce_sin(cos64[:, :].rearrange("p (a t) -> p a t", a=2), PI / 2)
        reduce_sin(sw64[:, :].rearrange("p (a t) -> p a t", a=2), 0.0)
        ts(sw64[:, :], sw64[:, :], cst_t[0:64, 1:2], None, ALU.mult, None, [B["tab"], B["cst"]], [B["tab"]])

        def rope(dst, dstBs, x, xBs, t0):
            cs = cos64[:, t0:t0 + TH]
            sw = sw64[:, t0:t0 + TH]
            tt(ropeT[:, :], x[0:64, :], cs, ALU.mult, xBs + [B["tab"]], [B["ropeT"]])
            tt(ropeU[0:32, :], x[32:64, :], sw[32:64, :], ALU.mult, xBs + [B["tab"]], [B["ropeU"]])
            tt(ropeU[32:64, :], x[0:32, :], sw[0:32, :], ALU.mult, xBs + [B["tab"]], [B["ropeU"]])
            tt(dst, ropeT[:, :], ropeU[:, :], ALU.add, [B["ropeT"], B["ropeU"]], dstBs)

        A1_ORDER = list(range(0, NB_A1))
        A2_ORDER = []
        for jj_ in range(16):
            A2_ORDER += [13 + 2 * jj_, 14 + 2 * jj_, 45 + jj_]
        A2_ORDER += list(range(61, 77))
        wlist = []
        for l in range(NL):
            for p in range(NPL):
                for j in A1_ORDER:
                    wlist.append(("in", l, j))
            for p in range(NPL):
                for j in A2_ORDER:
                    wlist.append(("in", l, j))
                for j in range(32):
                    wlist.append(("out", l, j))
        wstate = {"next": 0, "idx": 0, "main": 0, "sq": 0, "ab": 0, "prev": 0}

        def wprefetch(upto):
            while wstate["next"] <= min(upto, len(wlist) - 1):
                i = wstate["next"]
                kind, l, j = wlist[i]
                if kind == "in":
                    r0 = (l * NB_IN + j) * 128
                    src = w_in_r[r0:r0 + 128, :]
                else:
                    r0 = (l * 32 + j) * 128
                    src = w_out_r[r0:r0 + 128, :]
                S.dma("pool", wbuf[i % NW][:, :, :], src.rearrange("p (a n) -> p a n", a=4),
                      writes=[B_wbuf[i % NW]])
                wstate["next"] += 1

        def wtake(expect):
            i = wstate["idx"]
            assert wlist[i] == expect, (wlist[i], expect)
            wprefetch(i + NW - 1)
            wstate["idx"] += 1
            W = wbuf[i % NW][:, :, :].rearrange("p a (c n) -> p (a c) n", n=128)
            return W, B_wbuf[i % NW]

        def mainbank():
            b = wstate["main"] % 2
            wstate["main"] += 1
            return b

        ABANKS = [0, 1, 6, 7]

        def abank():
            b = ABANKS[wstate["ab"] % 4]
            wstate["ab"] += 1
            return b

        def sqbuf():
            k = wstate["sq"] % 2
            wstate["sq"] += 1
            return k

        O_LNG, O_QA, O_KVA, O_QNA, O_QNB, O_KNA, O_KNB, O_WDW, O_BDW, O_CLG, O_CLB = \
            0, 32, 40, 44, 45, 46, 47, 48, 48 + 496, 48 + 512, 48 + 528
        scale = 1.0 / math.sqrt(192.0)
        pend = []

        def defer_stat(f, delay=1):
            pend.append([delay, f])

        def tick_stat():
            for e in pend:
                e[0] -= 1
            while pend and pend[0][0] <= 0:
                pend.pop(0)[1]()

        def flush_stat():
            while pend:
                pend.pop(0)[1]()

        def phase0(l, lp, xsrc):
            t0 = lp * TH
            for rnd in range(2):
                for g4 in range(8):
                    so = (g4 % 2) * 4
                    st = stage[:, so:so + 4, :]
                    Bst = B_stage[so:so + 4]
                    src = xsrc[g4 * 512:(g4 + 1) * 512, t0:t0 + TH].rearrange("(c q) t -> q c t", q=128)
                    S.dma("sp", st, src, reads=[B_x[lp][g4 * 4 + c] for c in range(4)], writes=Bst)
                    for c8 in range(4):
                        c = g4 * 4 + c8
                        if rnd == 0:
                            k = sqbuf()
                            act(sqt[k][:, :], st[:, c8, :], AF.Square, [Bst[c8]], [B_sqt[k]])
                            mm(ps[2][:, :], ones[:, :], sqt[k][:, :], c == 0, c == 31,
                               [B_sqt[k], B["cst"]], [B_ps[2]])
                        else:
                            stt(hT[:, c, :], st[:, c8, :], par[:, O_LNG + c:O_LNG + c + 1], rbc[:, :],
                                ALU.mult, ALU.mult, [Bst[c8], B["par"], B["rbc"]], [B_hT[c]])
                if rnd == 0:
                    rstd_from(rbc[:, :], B["rbc"], ps[2][:, :], [B_ps[2]], float(D))
            S.dma("sp", hs[lp * D:(lp + 1) * D, :].rearrange("(c q) t -> q c t", q=128), hT[:, :, :],
                  reads=B_hT, writes=[B_hs[lp]])

        def block_mm(W, Bw, bank, M, rhs_of, rhsB_of, inc_last=True, cols=slice(0, TH)):
            for c in range(32):
                mm(ps[bank][0:M, cols], W[:, c, 0:M], rhs_of(c), c == 0, c == 31,
                   [Bw, rhsB_of(c)], [B_ps[bank]], inc=(c == 31 and inc_last) or (c == 31))

        def phaseA(l, lp, jlist):
            t0 = lp * TH
            for bi, j in enumerate(jlist):
                W, Bw = wtake(("in", l, j))
                if pcoll and bi % 6 == 2:
                    pcoll.pop(0)()
                a_bank = wstate["prev"]
                bank = abank()
                wstate["prev"] = bank
                M = 64 if j == 12 else 128
                block_mm(W, Bw, bank, M, lambda c: hT[:, c, :], lambda c: B_hT[c])
                is_conv = 13 <= j < 45
                if is_conv and lp == 0:
                    hb = 4 if (j - 13) % 2 == 0 else 5
                    block_mm(W, Bw, hb, 128, lambda c: hTh[:, c, :], lambda c: B["hTh"], cols=slice(0, HALO))
                tick_stat()
                P = ps[bank]
                BP = B_ps[bank]
                if j < 12:
                    isq = j < 8
                    ci = j if isq else j - 8
                    if isq:
                        stv = [stage[:, c_, :] for c_ in range(8)]
                        Bst = B_stage
                    else:
                        stv = [acc[0][:, :], acc[1][:, :], sig[0][:, :], sig[1][:, :]]
                        Bst = [B_acc[0], B_acc[1], B_sig[0], B_sig[1]]
                    nlast = 7 if isq else 3
                    acopy(stv[ci], P[:, :], [BP], [Bst[ci]])
                    k = sqbuf()
                    act(sqt[k][:, :], P[:, :], AF.Square, [BP], [B_sqt[k]])

                    def _stat(k=k, ci=ci, nlast=nlast):
                        mm(ps[2][:, :], ones[:, :], sqt[k][:, :], ci == 0, ci == nlast,
                           [B_sqt[k], B["cst"]], [B_ps[2]])
                    defer_stat(_stat, 1)
                    if ci == nlast:
                        flush_stat()
                        n = 1024.0 if isq else 512.0
                        rstd_from(rbc2[:, :], B["rbc2"], ps[2][:, :], [B_ps[2]], n)
                        dstT, Bd, og = (qcn, B_qcn, O_QA) if isq else (kvn, B_kvn, O_KVA)
                        for c2 in range(nlast + 1):
                            stt(dstT[:, c2, :], stv[c2], par[:, og + c2:og + c2 + 1], rbc2[:, :],
                                ALU.mult, ALU.mult, [Bst[c2], B["par"], B["rbc2"]], [Bd[c2]])
                        if isq:
                            S.dma("sp", qs[lp * 1024:(lp + 1) * 1024, :].rearrange("(c q) t -> q c t", q=128),
                                  qcn[:, :, :], reads=B_qcn, writes=[B_qs[lp]])
                elif j == 12:
                    acopy(kpe_f[:, :], P[0:64, :], [BP], [B["kpe_f"]])
                    k = sqbuf()
                    act(sqt[k][0:64, :], P[0:64, :], AF.Square, [BP], [B_sqt[k]])
                    mm(ps[2][:, :], ones[0:64, :], sqt[k][0:64, :], True, True, [B_sqt[k], B["cst"]], [B_ps[2]])
                    acopy(spe[:, :], ps[2][:, :], [B_ps[2]], [B["spe"]])
                    ts(kpe_f[:, :], kpe_f[:, :], par[0:64, O_KNB:O_KNB + 1], None, ALU.mult, None,
                       [B["kpe_f"], B["par"]], [B["kpe_f"]])
                    rope(kpe_rot[:, :], [B["kpe_rot"]], kpe_f, [B["kpe_f"]], t0)
                elif j < 45:
                    jj = (j - 13) // 2
                    if (j - 13) % 2 == 1:
                        ub = jj % 2
                        act(sig[ub][:, :], P[:, :], AF.Sigmoid, [BP], [B_sig[ub]])
                        if lp == 0:
                            act(sgh[:, :], ps[5][:, 0:HALO], AF.Sigmoid, [B_ps[5]], [B["sgh"]])
                            tt(uh[:, :], ps[4][:, 0:HALO], sgh[:, :], ALU.mult, [B_ps[4], B["sgh"]], [B["uh"]])
                            ts(ubuf[ub][:, 0:30], uh[:, 2:HALO], flg_t[:, 1:2], None, ALU.mult, None,
                               [B["uh"], B["cst"]], [B_ubuf[ub]])
                        else:
                            cp(ubuf[ub][:, 0:30], halo[:, jj, :], [B_halo[jj]], [B_ubuf[ub]])
                        tt(ubuf[ub][:, 30:30 + TH], ps[a_bank][:, :], sig[ub][:, :], ALU.mult,
                           [B_ps[a_bank], B_sig[ub]], [B_ubuf[ub]])
                        if lp < NPL - 1:
                            cp(halo[:, jj, :], ubuf[ub][:, TH:TH + 30], [B_ubuf[ub]], [B_halo[jj]])
                        wo = O_WDW + jj * 31
                        ts(acc[ub][:, :], ubuf[ub][:, 0:TH], par[:, wo:wo + 1],
                           par[:, O_BDW + jj:O_BDW + jj + 1], ALU.mult, ALU.add,
                           [B_ubuf[ub], B["par"]], [B_acc[ub]])
                        for tap in range(1, 31):
                            last = tap == 30
                            o_ap = catT[:, 16 + jj, :] if last else acc[ub][:, :]
                            stt(o_ap, ubuf[ub][:, tap:tap + TH], par[:, wo + tap:wo + tap + 1], acc[ub][:, :],
                                ALU.mult, ALU.add, [B_ubuf[ub], B["par"], B_acc[ub]],
                                [B_cat[16 + jj]] if last else [B_acc[ub]], nosync=(tap > 1))
                        kk = {}

                        def _sq(jj=jj, kk=kk):
                            kk["k"] = sqbuf()
                            act(sqt[kk["k"]][:, :], catT[:, 16 + jj, :], AF.Square, [B_cat[16 + jj]], [B_sqt[kk["k"]]])

                        def _stat(jj=jj, kk=kk):
                            k = kk["k"]
                            mm(ps[2][:, :], ones[:, :], catT[:, 16 + jj, :], jj == 0, jj == 15,
                               [B_cat[16 + jj], B["cst"]], [B_ps[2]])
                            mm(ps[3][:, :], ones[:, :], sqt[k][:, :], jj == 0, jj == 15,
                               [B_sqt[k], B["cst"]], [B_ps[3]])
                        defer_stat(_sq, 3)
                        defer_stat(_stat, 4)
                        if jj == 15:
                            flush_stat()
                            ts(mu_bc[:, :], ps[2][:, :], 1.0 / 2048, None, ALU.mult, None, [B_ps[2]], [B["mu"]])
                            tt(rs_bc[:, :], mu_bc[:, :], mu_bc[:, :], ALU.mult, [B["mu"]], [B["rs"]])
                            stt(rs_bc[:, :], ps[3][:, :], 1.0 / 2048, rs_bc[:, :], ALU.mult, ALU.subtract,
                                [B_ps[3], B["rs"]], [B["rs"]])
                            rstd_from(rs_bc[:, :], B["rs"], rs_bc[:, :], [B["rs"]], 1.0)
                elif j < 61:
                    h = j - 45
                    act(catT[:, h, :], P[:, :], AF.Silu, [BP], [B_cat[h]])
                else:
                    jj = j - 61
                    tb = jj % 2
                    act(sig[tb][:, :], P[:, :], AF.Silu, [BP], [B_sig[tb]])
                    tt(tmpf[tb][:, :], catT[:, 16 + jj, :], mu_bc[:, :], ALU.subtract,
                       [B_cat[16 + jj], B["mu"]], [B_tmpf[tb]])
                    tt(tmpf[tb][:, :], tmpf[tb][:, :], rs_bc[:, :], ALU.mult, [B_tmpf[tb], B["rs"]], [B_tmpf[tb]])
                    act(tmpf[tb][:, :], tmpf[tb][:, :], AF.Silu, [B_tmpf[tb], B["par"]], [B_tmpf[tb]],
                        bias=par[:, O_CLB + jj:O_CLB + jj + 1], scale=par[:, O_CLG + jj:O_CLG + jj + 1])
                    tt(catT[:, 16 + jj, :], tmpf[tb][:, :], sig[tb][:, :], ALU.mult,
                       [B_tmpf[tb], B_sig[tb]], [B_cat[16 + jj]])
            flush_stat()

        def kv_stage(l, lp):
            t0 = lp * TH
            rb = [rbc2, rbc]
            Brb = [B["rbc2"], B["rbc"]]
            sbank = [3, 2]

            def load_wk(h):
                r0 = (l * NH + h) * 128
                S.dma("pool", wk[h % 2][:, :, :], wk_r[r0:r0 + 128, :].rearrange("p (c n) -> p c n", c=4),
                      writes=[B_wk[h % 2]])

            def v_group(g):
                r0 = (l * 4 + g) * 128
                S.dma("pool", wv[:, :, :], wv_r[r0:r0 + 128, :].rearrange("p (c n) -> p c n", c=4),
                      writes=[B_wv])
                for tt_ in range(4):
                    bank = 6 + tt_ % 2
                    for c in range(4):
                        mm(ps[bank][:, :], kvn[:, c, tt_ * 128:(tt_ + 1) * 128], wv[:, c, :], c == 0, c == 3,
                           [B_kvn[c], B_wv], [B_ps[bank]], inc=(c == 3))
                    acopy(vtmp[:, tt_, :], ps[bank][:, :], [B_ps[bank]], [Bv_vtmp[tt_]])
                for k2 in range(2):
                    pr = 2 * g + k2
                    S.dma("sp", vcp_rows(pr, t0, t0 + TH).rearrange("(t q) d -> q t d", q=128),
                          vtmp[:, :, k2 * 256:(k2 + 1) * 256], reads=Bv_vtmp, writes=[B_vc[pr]])

            KB = [0, 1, 4, 5]

            def k_mm(h):
                WK = wk[h % 2]
                bk = KB[h % 4]
                for c in range(4):
                    mm(ps[bk][:, :], WK[:, c, :], kvn[:, c, :], c == 0, c == 3,
                       [B_wk[h % 2], B_kvn[c]], [B_ps[bk]], inc=(c == 3))
                k = sqbuf()
                act(sqt[k][:, :], ps[bk][:, :], AF.Square, [B_ps[bk]], [B_sqt[k]])
                return bk, k

            def k_chain1(h, bk, k):
                i2 = h % 2
                sbk = sbank[i2]
                mm(ps[sbk][:, :], ones[:, :], sqt[k][:, :], True, True, [B_sqt[k], B["cst"]], [B_ps[sbk]])
                tt(rb[i2][:, :], ps[sbk][:, :], spe[:, :], ALU.add, [B_ps[sbk], B["spe"]], [Brb[i2]])
                rstd_from(rb[i2][:, :], Brb[i2], rb[i2][:, :], [Brb[i2]], 192.0)

            def k_chain2(h, bk, k):
                i2 = h % 2
                i4 = h % 4
                stt(ktmp_a[i4], ps[bk][:, :], par[:, O_KNA:O_KNA + 1], rb[i2][:, :], ALU.mult, ALU.mult,
                    [B_ps[bk], B["par"], Brb[i2]], [B_hT[4 + i4]])
                tt(ktmp_b[i4], kpe_rot[:, :], rb[i2][0:64, :], ALU.mult,
                   [B["kpe_rot"], Brb[i2]], [B_hT[8 + i4]])
                S.dma("sp", kc_a_rows(h)[:, t0:t0 + TH], ktmp_a[i4], reads=[B_hT[4 + i4]], writes=[B_kc[h]])
                S.dma("sp", kc_b[h * 64:(h + 1) * 64, t0:t0 + TH], ktmp_b[i4], reads=[B_hT[8 + i4]], writes=[B_kc[h]])

            load_wk(0)
            load_wk(1)
            kq = {0: k_mm(0)}
            kq[1] = k_mm(1)
            load_wk(2)
            k_chain1(0, *kq[0])
            for h in range(NH):
                if h % 4 == 0:
                    v_group(h // 4)
                if h + 2 < NH:
                    kq[h + 2] = k_mm(h + 2)
                    if h + 3 < NH:
                        load_wk(h + 3)
                if h + 1 < NH:
                    k_chain1(h + 1, *kq[h + 1])
                k_chain2(h, *kq.pop(h))

        pcoll = []

        def exchange(l):
            S.coll(xh_t.ap().opt(), xh_gt.ap().opt(), groups, reads=[B["xh"]], writes=[B["xh_g"]])
            for i in range(2):
                pcoll.append(lambda i=i: S.coll(kc_a_t[i].ap().opt(), kc_a_gt[i].ap().opt(), groups,
                                                reads=B_kc, writes=[B["kc_g"]]))
            pcoll.append(lambda: S.coll(kc_b_t.ap().opt(), kc_b_gt.ap().opt(), groups, reads=B_kc, writes=[B["kc_g"]]))
            for i in range(2):
                pcoll.append(lambda i=i: S.coll(vcp_t[i].ap().opt(), vcp_gt[i].ap().opt(), groups,
                                                reads=B_vc, writes=[B["vc_g"]]))

        def flush_coll():
            while pcoll:
                pcoll.pop(0)()

        def halo_prep(l):
            xhs = stage[:, 0:2, :].rearrange("p a (c t) -> p (a c) t", t=HALO)
            Bst = B_stage[0:2]
            S.dma("sp", xhs, xh_g[0:D, :].rearrange("(c q) t -> q c t", q=128), reads=[B["xh_g"]], writes=Bst)
            sqh = sqt[0][:, :].rearrange("p (c t) -> p c t", t=HALO)
            for half in range(2):
                act(sqh, xhs[:, half * 16:(half + 1) * 16, :], AF.Square, Bst, [B_sqt[0]])
                for c in range(16):
                    cc = half * 16 + c
                    mm(ps[2][:, 0:HALO], ones[:, :], sqh[:, c, :], cc == 0, cc == 31, [B_sqt[0], B["cst"]], [B_ps[2]])
            rstd_from(rh[:, :], B["rh"], ps[2][:, 0:HALO], [B_ps[2]], float(D))
            for c in range(32):
                stt(hTh[:, c, :], xhs[:, c, :], par[:, O_LNG + c:O_LNG + c + 1], rh[:, :],
                    ALU.mult, ALU.mult, Bst + [B["par"], B["rh"]], [B["hTh"]])

        def attention(l, lp):
            t0 = lp * TH
            nown = (t0 // 128) + 4
            nkt = 8 + nown

            def load_wq(h):
                r0 = (l * NH + h) * 128
                S.dma("pool", wq[h % 2][:, :, :], wq_r[r0:r0 + 128, :].rearrange("p (c n) -> p c n", c=8),
                      writes=[B_wq[h % 2]])

            def prepA(h):
                i2 = h % 2
                S.dma("sp", ktw_a[i2][:, 0:TOK], kc_a_g_rows(h), reads=[B["kc_g"]], writes=Bv_kta[i2])
                S.dma("sp", ktw_a[i2][:, TOK:TOK + t0 + TH], kc_a_rows(h)[:, 0:t0 + TH],
                      reads=[B_kc[h]], writes=Bv_kta[i2])
                S.dma("sp", ktw_b[i2][:, 0:TOK], kc_b_g[h * 64:(h + 1) * 64, :], reads=[B["kc_g"]], writes=Bv_ktb[i2])
                S.dma("sp", ktw_b[i2][:, TOK:TOK + t0 + TH], kc_b[h * 64:(h + 1) * 64, 0:t0 + TH],
                      reads=[B_kc[h]], writes=Bv_ktb[i2])
                if h % 2 == 0:
                    pr = h // 2
                    vb = pr % 2
                    S.dma("sp", vgw[vb][:, 0:8, :], vcp_g_rows(pr).rearrange("(t q) d -> q t d", q=128),
                          reads=[B["vc_g"]], writes=Bv_vgw[vb])
                    S.dma("sp", vgw[vb][:, 8:8 + nown, :],
                          vcp_rows(pr, 0, t0 + TH).rearrange("(t q) d -> q t d", q=128),
                          reads=[B_vc[pr]], writes=Bv_vgw[vb])
                WQ = wq[i2]
                bqa = mainbank()
                for c in range(8):
                    mm(ps[bqa][:, :], WQ[:, c, 0:128], qcn[:, c, :], c == 0, c == 7,
                       [B_wq[i2], B_qcn[c]], [B_ps[bqa]], inc=(c == 7))
                bqb = mainbank()
                for c in range(8):
                    mm(ps[bqb][0:64, :], WQ[:, c, 128:192], qcn[:, c, :], c == 0, c == 7,
                       [B_wq[i2], B_qcn[c]], [B_ps[bqb]], inc=(c == 7))
                act(sqt[0][:, :], ps[bqa][:, :], AF.Square, [B_ps[bqa]], [B_sqt[0]])
                act(sqt[1][0:64, :], ps[bqb][0:64, :], AF.Square, [B_ps[bqb]], [B_sqt[1]])
                return bqa, bqb

            def prepB(h, bqa, bqb):
                i2 = h % 2
                mm(ps[2][:, :], ones[:, :], sqt[0][:, :], True, False, [B_sqt[0], B["cst"]], [B_ps[2]], inc=False)
                mm(ps[2][:, :], ones[0:64, :], sqt[1][0:64, :], False, True, [B_sqt[1], B["cst"]], [B_ps[2]])
                rstd_from(rbc[:, :], B["rbc"], ps[2][:, :], [B_ps[2]], 192.0)
                stt(qt_a[i2][:, :], ps[bqa][:, :], par[:, O_QNA:O_QNA + 1], rbc[:, :], ALU.mult, ALU.mult,
                    [B_ps[bqa], B["par"], B["rbc"]], [B_qta[i2]])
                stt(qb_f[:, :], ps[bqb][0:64, :], par[0:64, O_QNB:O_QNB + 1], rbc[0:64, :], ALU.mult, ALU.mult,
                    [B_ps[bqb], B["par"], B["rbc"]], [B["qb_f"]])
                rope(qt_b[i2][:, :], [B_qtb[i2]], qb_f, [B["qb_f"]], t0)

            def tile_geo(kt):
                jd = kt - 8 - t0 // 128
                q0 = jd * 128 if jd > 0 else 0
                return jd, q0, (4, 5, 3)[kt % 3], kt % 3

            def score_mm(h, kt):
                i2 = h % 2
                jd, q0, sb_, pi_ = tile_geo(kt)
                mm(ps[sb_][:, q0:TH], ktw_a[i2][:, kt * 128:(kt + 1) * 128], qt_a[i2][:, q0:TH], True, False,
                   Bv_kta[i2] + [B_qta[i2]], [B_ps[sb_]], inc=False)
                mm(ps[sb_][:, q0:TH], ktw_b[i2][:, kt * 128:(kt + 1) * 128], qt_b[i2][:, q0:TH], False, True,
                   Bv_ktb[i2] + [B_qtb[i2]], [B_ps[sb_]])

            def score_rest(h, kt):
                i2 = h % 2
                vb = (h // 2) % 2
                vs = (h % 2) * 128
                own = kt >= 8
                jd, q0, sb_, pi_ = tile_geo(kt)
                bias = -8.0 if own else flg_t[:, 0:1]
                act(pT[pi_][:, q0:TH], ps[sb_][:, q0:TH], AF.Exp, [B_ps[sb_], B["cst"]], [B_pT[pi_]],
                    bias=bias, scale=scale)
                if jd >= 0:
                    tt(pT[pi_][:, q0:q0 + 128], pT[pi_][:, q0:q0 + 128], tri[:, :], ALU.mult,
                       [B_pT[pi_], B["cst"]], [B_pT[pi_]])
                mm(ps[6][:, q0:TH], vgw[vb][:, kt, vs:vs + 128], pT[pi_][:, q0:TH],
                   kt == 0, kt == nkt - 1, Bv_vgw[vb] + [B_pT[pi_]], [B_ps[6]], inc=(kt == nkt - 1))
                mm(ps[7][:, q0:TH], ones[:, :], pT[pi_][:, q0:TH],
                   kt == 0, kt == nkt - 1, [B["cst"], B_pT[pi_]], [B_ps[7]])

            def finish(h):
                tb = h % 2
                act(tmpf[tb][:, :], ps[7][:, :], AF.Ln, [B_ps[7]], [B_tmpf[tb]])
                act(tmpf[tb][:, :], tmpf[tb][:, :], AF.Exp, [B_tmpf[tb]], [B_tmpf[tb]], scale=-1.0)
                tt(tmpf[tb][:, :], ps[6][:, :], tmpf[tb][:, :], ALU.mult, [B_ps[6], B_tmpf[tb]], [B_tmpf[tb]])
                tt(catT[:, h, :], tmpf[tb][:, :], catT[:, h, :], ALU.mult, [B_tmpf[tb], B_cat[h]], [B_cat[h]])

            load_wq(0)
            load_wq(1)
            ba, bb = prepA(0)
            prepB(0, ba, bb)
            for h in range(NH):
                nxt = None
                if h + 1 < NH:
                    nxt = prepA(h + 1)
                score_mm(h, 0)
                score_mm(h, 1)
                for kt in range(nkt):
                    if kt + 2 < nkt:
                        score_mm(h, kt + 2)
                    score_rest(h, kt)
                    if kt == 2 and nxt is not None:
                        prepB(h + 1, *nxt)
                        if h + 2 < NH:
                            load_wq(h + 2)
                finish(h)

        def phaseC(l, lp, xsrc):
            t0 = lp * TH
            for jo in range(32):
                W, Bw = wtake(("out", l, jo))
                bank = abank()
                xi = jo % 3
                S.dma("sp", xin[xi], xsrc[jo * 128:(jo + 1) * 128, t0:t0 + TH],
                      reads=[B_x[lp][jo]], writes=[B_xin[xi]])
                block_mm(W, Bw, bank, 128, lambda c: catT[:, c, :], lambda c: B_cat[c])
                tt(xin[xi], ps[bank][:, :], xin[xi], ALU.add, [B_ps[bank], B_xin[xi]], [B_xin[xi]])
                S.dma("sp", out[jo * 128:(jo + 1) * 128, t0:t0 + TH], xin[xi],
                      reads=[B_xin[xi]], writes=[B_x[lp][jo]])

        wprefetch(NW - 1)
        for l in range(NL):
            xsrc = xT if l == 0 else out
            S.dma("sp", par[:, :], sp_r[l * 128:(l + 1) * 128, :], writes=[B["par"]])
            for lp in range(NPL):
                phase0(l, lp, xsrc)
                if lp == NPL - 1:
                    for q4 in range(4):
                        S.dma("sp", xh[q4 * 1024:(q4 + 1) * 1024, :], xsrc[q4 * 1024:(q4 + 1) * 1024, TOK - HALO:TOK],
                              reads=B_x[lp][q4 * 8:(q4 + 1) * 8], writes=[B["xh"]])
                phaseA(l, lp, A1_ORDER)
                kv_stage(l, lp)
            exchange(l)
            for lp in range(NPL):
                S.dma("sp", hT[:, :, :], hs[lp * D:(lp + 1) * D, :].rearrange("(c q) t -> q c t", q=128),
                      reads=[B_hs[lp]], writes=B_hT)
                S.dma("sp", qcn[:, :, :], qs[lp * 1024:(lp + 1) * 1024, :].rearrange("(c q) t -> q c t", q=128),
                      reads=[B_qs[lp]], writes=B_qcn)
                if lp == 0:
                    halo_prep(l)
                phaseA(l, lp, A2_ORDER)
                flush_coll()
                attention(l, lp)
                phaseC(l, lp, xsrc)
        for p in range(NPL):
            S.wait_bufs("sp", B_x[p])
        print("program built: insts=%d, sems pe=%d act=%d dve=%d" % (S.n_inst, S.cnt["pe"], S.cnt["act"], S.cnt["dve"]))
    return nc


def _in_col_order():
    blocks = []
    blocks += [np.arange(i * 128, (i + 1) * 128) for i in range(8)]
    blocks += [np.arange(1024 + i * 128, 1024 + (i + 1) * 128) for i in range(4)]
    kpe = np.full(128, -1)
    kpe[:64] = np.arange(1536, 1600)
    blocks.append(kpe)
    for j in range(16):
        blocks.append(np.arange(3648 + j * 128, 3648 + (j + 1) * 128))
        blocks.append(np.arange(3648 + 2048 + j * 128, 3648 + 2048 + (j + 1) * 128))
    blocks += [np.arange(1600 + h * 128, 1600 + (h + 1) * 128) for h in range(16)]
    blocks += [np.arange(7744 + j * 128, 7744 + (j + 1) * 128) for j in range(16)]
    return blocks


def _block_layout(w, blocks):
    K = w.shape[0]
    kc = K // 128
    outp = np.zeros((len(blocks), 128, kc, 128), np.float32)
    w3 = w.reshape(kc, 128, w.shape[1])
    for j, cols in enumerate(blocks):
        valid = cols >= 0
        sub = w3[:, :, cols[valid]]
        outp[j, :, :, :valid.sum()] = sub.transpose(1, 0, 2)
    return outp.reshape(len(blocks) * 128, kc * 128)


def prepare_weights(NL, ln_g, w_in, q_a_norm, w_q_up, kv_a_norm, w_kv_up, q_norm, k_norm,
                    w_dw, b_dw, conv_ln_g, conv_ln_b, w_out):
    blocks_in = _in_col_order()
    blocks_out = [np.arange(j * 128, (j + 1) * 128) for j in range(32)]
    w_in_r = np.concatenate([_block_layout(np.asarray(w_in[l]), blocks_in) for l in range(NL)], 0)
    w_out_r = np.concatenate([_block_layout(np.asarray(w_out[l]), blocks_out) for l in range(NL)], 0)
    wq_l, wk_l, wv_l, sp_l = [], [], [], []
    for l in range(NL):
        wq = np.asarray(w_q_up[l]).reshape(8, 128, NH, 192)
        wq_l.append(wq.transpose(2, 1, 0, 3).reshape(NH * 128, 8 * 192))
        wkv = np.asarray(w_kv_up[l]).reshape(4, 128, NH, 256)
        wk_l.append(wkv[:, :, :, :128].transpose(2, 1, 0, 3).reshape(NH * 128, 4 * 128))
        wv = wkv[:, :, :, 128:].reshape(4, 128, 4, 4, 128)
        wv_l.append(wv.transpose(2, 1, 0, 3, 4).reshape(4 * 128, 4 * 512))
        sp = np.zeros((128, NSP), np.float32)
        sp[:, 0:32] = np.asarray(ln_g[l]).reshape(32, 128).T
        sp[:, 32:40] = np.asarray(q_a_norm[l]).reshape(8, 128).T
        sp[:, 40:44] = np.asarray(kv_a_norm[l]).reshape(4, 128).T
        sp[:, 44] = np.asarray(q_norm[l])[:128]
        sp[:64, 45] = np.asarray(q_norm[l])[128:]
        sp[:, 46] = np.asarray(k_norm[l])[:128]
        sp[:64, 47] = np.asarray(k_norm[l])[128:]
        sp[:, 48:48 + 496] = np.asarray(w_dw[l]).reshape(31, 16, 128).transpose(2, 1, 0).reshape(128, 496)
        sp[:, 544:560] = np.asarray(b_dw[l]).reshape(16, 128).T
        sp[:, 560:576] = np.asarray(conv_ln_g[l]).reshape(16, 128).T
        sp[:, 576:592] = np.asarray(conv_ln_b[l]).reshape(16, 128).T
        sp_l.append(sp)
    cst = np.zeros((128, 130), np.float32)
    half = 32
    inv_freq = (10000.0 ** (-np.arange(half, dtype=np.float32) / half)).astype(np.float32)
    cst[:64, 0] = np.concatenate([inv_freq, inv_freq])
    cst[:32, 1] = 1.0
    cst[32:64, 1] = -1.0
    kk = np.arange(128)
    cst[:, 2:130] = (kk[None, :] >= kk[:, None]).astype(np.float32)
    return dict(
        w_in_r=np.ascontiguousarray(w_in_r), w_out_r=np.ascontiguousarray(w_out_r),
        wq_r=np.ascontiguousarray(np.concatenate(wq_l, 0)), wk_r=np.ascontiguousarray(np.concatenate(wk_l, 0)),
        wv_r=np.ascontiguousarray(np.concatenate(wv_l, 0)), sp_r=np.ascontiguousarray(np.concatenate(sp_l, 0)),
        cst=cst)


def run(NL, n_cores, x, positions, wd):
    nc = build_program(NL, n_cores)
    in_maps = []
    for c in range(n_cores):
        b, half = c // 2, c % 2
        m = dict(wd)
        m["xT"] = np.ascontiguousarray(np.asarray(x[b])[half * TOK:(half + 1) * TOK].T)
        m["pos"] = np.ascontiguousarray(np.asarray(positions[b])[half * TOK:(half + 1) * TOK].reshape(1, TOK).astype(np.int32))
        f = np.zeros((128, 2), np.float32)
        f[:, 0] = -8.0 if half == 1 else -30000.0
        f[:, 1] = 1.0 if half == 1 else 0.0
        m["flg"] = f
        in_maps.append(m)
    res = run_bass_kernel_spmd(nc, in_maps, core_ids=list(range(n_cores)))
    y = np.zeros((n_cores // 2, SEQ, D), np.float32)
    for c in range(n_cores):
        b, half = c // 2, c % 2
        y[b, half * TOK:(half + 1) * TOK] = res.results[c]["out"].T
    return y


def kernel(x, positions, ln_g, w_in, q_a_norm, w_q_up, kv_a_norm, w_kv_up, q_norm, k_norm,
           w_dw, b_dw, conv_ln_g, conv_ln_b, w_out):
    wd = prepare_weights(DEPTH, ln_g, w_in, q_a_norm, w_q_up, kv_a_norm, w_kv_up, q_norm, k_norm,
                         w_dw, b_dw, conv_ln_g, conv_ln_b, w_out)
    y = run(DEPTH, N_CORES, np.asarray(x), np.asarray(positions), wd)
    return y.astype(np.float32)
```

```python
from contextlib import ExitStack
import math
import numpy as np
import concourse.bass as bass
import concourse.mybir as mybir
from concourse.bass_utils import run_bass_kernel_spmd

F32 = mybir.dt.float32
BF16 = mybir.dt.bfloat16
I32 = mybir.dt.int32
ALU = mybir.AluOpType
AF = mybir.ActivationFunctionType

D = 4096
SEQ = 2048
DEPTH = 4
TOK = 1024
TH = 512
NPL = TOK // TH
NH = 16
EPS = 1e-6
NB_IN = 77
NB_A1 = 13
NSP = 592
SAME_ENGINE_SYNC = True
N_DMA_SEMS = 24
N_CORES = 8
HALO = 32


class Buf:
    __slots__ = ("name", "w", "rs")

    def __init__(self, name):
        self.name = name
        self.w = None
        self.rs = []


class Sched:
    def __init__(self, nc, es):
        self.nc = nc
        self.eng = {"pe": nc.tensor, "act": nc.scalar, "dve": nc.vector,
                    "pool": nc.gpsimd, "sp": nc.sync}
        self.sem = {}
        self.cnt = {}
        for k in self.eng:
            self.sem[k] = es.enter_context(nc.semaphore("s_" + k))
            self.cnt[k] = 0
        self.dsem = [es.enter_context(nc.semaphore("d%d" % i)) for i in range(N_DMA_SEMS)]
        self.dcnt = [0] * N_DMA_SEMS
        self.dnext = 0
        self.csem = es.enter_context(nc.semaphore("s_coll"))
        self.ccnt = 0
        self.seen = {k: {} for k in self.eng}
        self.n_inst = 0
        self.nosync = False

    def _semof(self, key):
        if isinstance(key, tuple):
            return self.dsem[key[1]]
        if key == "coll":
            return self.csem
        return self.sem[key]

    def _wait(self, E, tok):
        key, val = tok
        if key == E and (E == "pe" or not SAME_ENGINE_SYNC or self.nosync):
            return
        if self.seen[E].get(key, 0) >= val:
            return
        self.seen[E][key] = val
        self.eng[E].wait_ge(self._semof(key), val)
        self.n_inst += 1

    def _deps(self, E, reads, writes):
        for b in reads:
            if b.w is not None:
                self._wait(E, b.w)
        for b in writes:
            if b.w is not None:
                self._wait(E, b.w)
            for t in b.rs:
                self._wait(E, t)

    def _record(self, tok, reads, writes):
        for b in reads:
            b.rs.append(tok)
        for b in writes:
            b.w = tok
            b.rs = []

    def op(self, E, fn, reads=(), writes=(), inc=True, nosync=False):
        self.nosync = nosync
        self._deps(E, reads, writes)
        self.nosync = False
        inst = fn(self.eng[E])
        tok = (E, self.cnt[E] + 1)
        if inc:
            inst.then_inc(self.sem[E], 1)
            self.cnt[E] += 1
        self._record(tok, reads, writes)
        self.n_inst += 1
        return tok

    def dma(self, Q, out, in_, reads=(), writes=()):
        self._deps(Q, reads, writes)
        i = self.dnext
        self.dnext = (self.dnext + 1) % N_DMA_SEMS
        key = ("d", i)
        if self.dcnt[i] > 0:
            self._wait(Q, (key, self.dcnt[i]))
        inst = self.eng[Q].dma_start(out=out, in_=in_)
        self.dcnt[i] += 16
        inst.then_inc(self.dsem[i], 16)
        tok = (key, self.dcnt[i])
        self._record(tok, reads, writes)
        self.n_inst += 1
        return tok

    def coll(self, in_ap, out_ap, groups, reads, writes):
        self._deps("pool", reads, writes)
        inst = self.nc.gpsimd.collective_compute("AllGather", ALU.bypass, replica_groups=groups,
                                                 ins=[in_ap], outs=[out_ap])
        inst.then_inc(self.csem)
        self.ccnt += 1
        tok = ("coll", self.ccnt)
        self._record(tok, reads, writes)
        self.n_inst += 1
        return tok

    def wait_bufs(self, E, bufs):
        for b in bufs:
            if b.w is not None:
                self._wait(E, b.w)
            for t in b.rs:
                self._wait(E, t)


def build_program(NL, n_cores=N_CORES):
    nc = bass.Bass("TRN2", target_bir_lowering=False)
    dt_in = lambda name, shape, dt=F32: nc.dram_tensor(name, shape, dt, kind="ExternalInput").ap()
    xT = dt_in("xT", [D, TOK])
    pos = dt_in("pos", [1, TOK], I32)
    w_in_r = dt_in("w_in_r", [NL * NB_IN * 128, 4096])
    w_out_r = dt_in("w_out_r", [NL * 32 * 128, 4096])
    wq_r = dt_in("wq_r", [NL * NH * 128, 8 * 192])
    wk_r = dt_in("wk_r", [NL * NH * 128, 4 * 128])
    wv_r = dt_in("wv_r", [NL * 4 * 128, 4 * 512])
    sp_r = dt_in("sp_r", [NL * 128, NSP])
    cst = dt_in("cst", [128, 130])
    flg = dt_in("flg", [128, 2])
    out = nc.dram_tensor("out", [D, TOK], F32, kind="ExternalOutput").ap()
    kc_a_t = [nc.dram_tensor("kc_a%d" % i, [8 * 128, TOK], BF16) for i in range(2)]
    kc_b_t = nc.dram_tensor("kc_b", [NH * 64, TOK], BF16)
    vcp_t = [nc.dram_tensor("vcp%d" % i, [4 * TOK, 256], BF16) for i in range(2)]
    xh_t = nc.dram_tensor("xh", [D, HALO], F32)
    kc_a_gt = [nc.dram_tensor("kc_a_g%d" % i, [2 * 8 * 128, TOK], BF16) for i in range(2)]
    kc_b_gt = nc.dram_tensor("kc_b_g", [2 * NH * 64, TOK], BF16)
    vcp_gt = [nc.dram_tensor("vcp_g%d" % i, [2 * 4 * TOK, 256], BF16) for i in range(2)]
    xh_gt = nc.dram_tensor("xh_g", [2 * D, HALO], F32)
    kc_b, xh = kc_b_t.ap(), xh_t.ap()
    kc_b_g, xh_g = kc_b_gt.ap(), xh_gt.ap()

    def kc_a_rows(h):
        return kc_a_t[h // 8].ap()[(h % 8) * 128:(h % 8 + 1) * 128, :]

    def kc_a_g_rows(h):
        return kc_a_gt[h // 8].ap()[(h % 8) * 128:(h % 8 + 1) * 128, :]

    def vcp_rows(pr, r0, r1):
        return vcp_t[pr // 4].ap()[(pr % 4) * TOK + r0:(pr % 4) * TOK + r1, :]

    def vcp_g_rows(pr):
        return vcp_gt[pr // 4].ap()[(pr % 4) * TOK:(pr % 4 + 1) * TOK, :]
    hs = nc.dram_tensor("hs", [NPL * D, TH], BF16).ap()
    qs = nc.dram_tensor("qs", [NPL * 1024, TH], BF16).ap()
    groups = [[2 * i, 2 * i + 1] for i in range(n_cores // 2)]

    with ExitStack() as es:
        S = Sched(nc, es)

        def sb(name, shape, dt):
            return es.enter_context(nc.sbuf_tensor(name, shape, dt))

        hT = sb("hT", [128, 32, TH], BF16)
        catT = sb("catT", [128, 32, TH], BF16)
        NW = 4
        wbuf = [sb("wbuf%d" % i, [128, 2, 2048], BF16) for i in range(NW)]
        wq = [sb("wq%d" % i, [128, 8, 192], BF16) for i in range(2)]
        wk = [sb("wk%d" % i, [128, 4, 128], BF16) for i in range(2)]
        wv = sb("wv", [128, 4, 512], BF16)
        qcn = sb("qcn", [128, 8, TH], BF16)
        kvn = sb("kvn", [128, 4, TH], BF16)
        stage = sb("stage", [128, 8, TH], F32)
        sqt = [sb("sqt%d" % i, [128, TH], BF16) for i in range(2)]
        rbc = sb("rbc", [128, TH], F32)
        rbc2 = sb("rbc2", [128, TH], F32)
        spe = sb("spe", [128, TH], F32)
        kpe_f = sb("kpe_f", [64, TH], F32)
        kpe_rot = sb("kpe_rot", [64, TH], F32)
        ropeT = sb("ropeT", [64, TH], F32)
        ropeU = sb("ropeU", [64, TH], F32)
        qb_f = sb("qb_f", [64, TH], F32)
        cos64 = sb("cos64", [64, TOK], F32)
        sw64 = sb("sw64", [64, TOK], F32)
        par = sb("par", [128, NSP], F32)
        cst_t = sb("cst_t", [128, 2], F32)
        flg_t = sb("flg_t", [128, 2], F32)
        tri = sb("tri", [128, 128], BF16)
        ones = sb("ones", [128, 128], BF16)
        halo = sb("halo", [128, 16, 30], F32)
        hTh = sb("hTh", [128, 32, HALO], BF16)
        rh = sb("rh", [128, HALO], F32)
        sgh = sb("sgh", [128, HALO], F32)
        uh = sb("uh", [128, HALO], F32)
        ubuf = [sb("ubuf%d" % i, [128, 30 + TH], F32) for i in range(2)]
        sig = [sb("sig%d" % i, [128, TH], F32) for i in range(2)]
        acc = [sb("acc%d" % i, [128, TH], F32) for i in range(2)]
        tmpf = [sb("tmpf%d" % i, [128, TH], F32) for i in range(2)]
        mu_bc = sb("mu_bc", [128, TH], F32)
        rs_bc = sb("rs_bc", [128, TH], F32)
        qt_a = [sb("qt_a%d" % i, [128, TH], BF16) for i in range(2)]
        qt_b = [sb("qt_b%d" % i, [64, TH], BF16) for i in range(2)]
        pT = [sb("pT%d" % i, [128, TH], BF16) for i in range(3)]
        ps = [es.enter_context(nc.psum_tensor("ps%d" % i, [128, TH], F32)) for i in range(8)]

        B_hT = [Buf("hT%d" % c) for c in range(32)]
        B_cat = [Buf("cat%d" % c) for c in range(32)]
        B_wbuf = [Buf("wbuf%d" % i) for i in range(NW)]
        B_wq = [Buf("wq%d" % i) for i in range(2)]
        B_wk = [Buf("wk%d" % i) for i in range(2)]
        B_wv = Buf("wv")
        B_qcn = [Buf("qcn%d" % c) for c in range(8)]
        B_kvn = [Buf("kvn%d" % c) for c in range(4)]
        B_stage = [Buf("stage%d" % c) for c in range(8)]
        B_sqt = [Buf("sqt%d" % i) for i in range(2)]
        B = {k: Buf(k) for k in ["rbc", "rbc2", "spe", "kpe_f", "kpe_rot", "ropeT", "ropeU", "qb_f",
                                 "tab", "par", "cst", "mu", "rs", "hTh", "rh", "sgh", "uh",
                                 "xh", "xh_g", "kc_g", "vc_g"]}
        B_halo = [Buf("halo%d" % j) for j in range(16)]
        B_ubuf = [Buf("ubuf%d" % i) for i in range(2)]
        B_sig = [Buf("sig%d" % i) for i in range(2)]
        B_acc = [Buf("acc%d" % i) for i in range(2)]
        B_tmpf = [Buf("tmpf%d" % i) for i in range(2)]
        B_qta = [Buf("qta%d" % i) for i in range(2)]
        B_qtb = [Buf("qtb%d" % i) for i in range(2)]
        B_pT = [Buf("pT%d" % i) for i in range(3)]
        B_ps = [Buf("ps%d" % i) for i in range(8)]
        B_xin = B_stage[0:3]
        xin = [stage[:, i, :] for i in range(3)]
        B_x = [[Buf("x_%d_%d" % (p, j)) for j in range(32)] for p in range(NPL)]
        B_kc = [Buf("kc%d" % h) for h in range(NH)]
        B_vc = [Buf("vc%d" % g) for g in range(8)]
        B_hs = [Buf("hs%d" % p) for p in range(NPL)]
        B_qs = [Buf("qs%d" % p) for p in range(NPL)]

        def flat(ap):
            return ap.rearrange("p a t -> p (a t)")
        vgw = [hT[:, 8 * i:8 * i + 8, :].rearrange("p a (k d) -> p (a k) d", d=256) for i in range(2)]
        Bv_vgw = [B_hT[8 * i:8 * i + 8] for i in range(2)]
        ktw_a = [flat(hT[:, 16 + 8 * i:20 + 8 * i, :]) for i in range(2)]
        ktw_b = [flat(hT[0:64, 20 + 8 * i:24 + 8 * i, :]) for i in range(2)]
        Bv_kta = [B_hT[16 + 8 * i:20 + 8 * i] for i in range(2)]
        Bv_ktb = [B_hT[20 + 8 * i:24 + 8 * i] for i in range(2)]
        vtmp = hT[:, 0:4, :]
        Bv_vtmp = B_hT[0:4]
        ktmp_a = [hT[:, 4 + i, :] for i in range(4)]
        ktmp_b = [hT[0:64, 8 + i, :] for i in range(4)]

        def mm(o, l, r, start, stop, reads, writes, inc=True):
            S.op("pe", lambda e: e.matmul(o, l, r, start=start, stop=stop), reads, writes, inc)

        def act(o, i, func, reads, writes, bias=0.0, scale=1.0):
            S.op("act", lambda e: e.activation(out=o, in_=i, func=func, bias=bias, scale=scale), reads, writes)

        def acopy(o, i, reads, writes):
            S.op("act", lambda e: e.copy(out=o, in_=i), reads, writes)

        def tt(o, a, b, op, reads, writes, E="dve"):
            S.op(E, lambda e: e.tensor_tensor(out=o, in0=a, in1=b, op=op), reads, writes)

        def stt(o, a, s, b, op0, op1, reads, writes, E="dve", nosync=False):
            S.op(E, lambda e: e.scalar_tensor_tensor(out=o, in0=a, scalar=s, in1=b, op0=op0, op1=op1), reads, writes,
                 nosync=nosync)

        def ts(o, a, s1, s2, op0, op1, reads, writes, E="dve"):
            if s2 is None:
                S.op(E, lambda e: e.tensor_scalar(out=o, in0=a, scalar1=s1, scalar2=None, op0=op0), reads, writes)
            else:
                S.op(E, lambda e: e.tensor_scalar(out=o, in0=a, scalar1=s1, scalar2=s2, op0=op0, op1=op1), reads, writes)

        def cp(o, i, reads, writes, E="dve"):
            S.op(E, lambda e: e.tensor_copy(out=o, in_=i), reads, writes)

        def recip(o, i, reads, writes):
            S.op("dve", lambda e: e.reciprocal(out=o, in_=i), reads, writes)

        def rstd_from(dst, dstB, src, srcBs, n):
            act(dst, src, AF.Ln, srcBs, [dstB], bias=EPS, scale=1.0 / n)
            act(dst, dst, AF.Exp, [dstB], [dstB], scale=-0.5)

        S.dma("sp", cst_t[:], cst[:, 0:2], writes=[B["cst"]])
        S.dma("sp", flg_t[:], flg[:, :], writes=[B["cst"]])
        S.dma("pool", tri[:], cst[:, 2:130], writes=[B["cst"]])
        S.op("dve", lambda e: e.memset(ones[:], 1.0), writes=[B["cst"]])
        PI = float(np.pi)
        C1 = 6.28125
        C2 = 2 * PI - C1
        posi = stage[0:64, 0:2, :].bitcast(I32)
        posf = stage[0:64, 2:4, :]
        ang_a = stage[0:64, 4:6, :]
        ang_k = stage[0:64, 6:8, :]
        SB_ = B_stage
        S.dma("sp", posi.rearrange("p a t -> p (a t)"), pos.partition_broadcast(64), writes=SB_)
        cp(posf, posi, SB_, SB_)

        def reduce_sin(dst, shift):
            ts(ang_a, posf, cst_t[0:64, 0:1], shift, ALU.mult, ALU.add, SB_ + [B["cst"]], SB_)
            ts(ang_k, ang_a, 1.0 / (2 * PI), None, ALU.mult, None, SB_, SB_)
            cp(posi, ang_k, SB_, SB_)
            cp(ang_k, posi, SB_, SB_)
            stt(ang_a, ang_k, -C1, ang_a, ALU.mult, ALU.add, SB_, SB_)
            stt(ang_a, ang_k, -C2, ang_a, ALU.mult, ALU.add, SB_, SB_)
            ts(ang_k, ang_a, PI, -2 * PI, ALU.is_gt, ALU.mult, SB_, SB_)
            tt(ang_a, ang_a, ang_k, ALU.add, SB_, SB_)
            ts(ang_k, ang_a, -PI, 2 * PI, ALU.is_lt, ALU.mult, SB_, SB_)
            tt(ang_a, ang_a, ang_k, ALU.add, SB_, SB_)
            act(dst, ang_a, AF.Sin, SB_, [B["tab"]])

        reduce_sin(cos64[:, :].rearrange("p (a t) -> p a t", a=2), PI / 2)
        reduce_sin(sw64[:, :].rearrange("p (a t) -> p a t", a=2), 0.0)
        ts(sw64[:, :], sw64[:, :], cst_t[0:64, 1:2], None, ALU.mult, None, [B["tab"], B["cst"]], [B["tab"]])

        def rope(dst, dstBs, x, xBs, t0):
            cs = cos64[:, t0:t0 + TH]
            sw = sw64[:, t0:t0 + TH]
            tt(ropeT[:, :], x[0:64, :], cs, ALU.mult, xBs + [B["tab"]], [B["ropeT"]])
            tt(ropeU[0:32, :], x[32:64, :], sw[32:64, :], ALU.mult, xBs + [B["tab"]], [B["ropeU"]])
            tt(ropeU[32:64, :], x[0:32, :], sw[0:32, :], ALU.mult, xBs + [B["tab"]], [B["ropeU"]])
            tt(dst, ropeT[:, :], ropeU[:, :], ALU.add, [B["ropeT"], B["ropeU"]], dstBs)

        A1_ORDER = list(range(0, NB_A1))
        A2_ORDER = []
        for jj_ in range(16):
            A2_ORDER += [13 + 2 * jj_, 14 + 2 * jj_, 45 + jj_]
        A2_ORDER += list(range(61, 77))
        wlist = []
        for l in range(NL):
            for p in range(NPL):
                for j in A1_ORDER:
                    wlist.append(("in", l, j))
            for p in range(NPL):
                for j in A2_ORDER:
                    wlist.append(("in", l, j))
                for j in range(32):
                    wlist.append(("out", l, j))
        wstate = {"next": 0, "idx": 0, "main": 0, "sq": 0, "ab": 0, "prev": 0}

        def wprefetch(upto):
            while wstate["next"] <= min(upto, len(wlist) - 1):
                i = wstate["next"]
                kind, l, j = wlist[i]
                if kind == "in":
                    r0 = (l * NB_IN + j) * 128
                    src = w_in_r[r0:r0 + 128, :]
                else:
                    r0 = (l * 32 + j) * 128
                    src = w_out_r[r0:r0 + 128, :]
                S.dma("pool", wbuf[i % NW][:, :, :], src.rearrange("p (a n) -> p a n", a=2),
                      writes=[B_wbuf[i % NW]])
                wstate["next"] += 1

        def wtake(expect):
            i = wstate["idx"]
            assert wlist[i] == expect, (wlist[i], expect)
            wprefetch(i + NW - 1)
            wstate["idx"] += 1
            W = wbuf[i % NW][:, :, :].rearrange("p a (c n) -> p (a c) n", n=128)
            return W, B_wbuf[i % NW]

        def mainbank():
            b = wstate["main"] % 2
            wstate["main"] += 1
            return b

        ABANKS = [0, 1, 6, 7]

        def abank():
            b = ABANKS[wstate["ab"] % 4]
            wstate["ab"] += 1
            return b

        def sqbuf():
            k = wstate["sq"] % 2
            wstate["sq"] += 1
            return k

        O_LNG, O_QA, O_KVA, O_QNA, O_QNB, O_KNA, O_KNB, O_WDW, O_BDW, O_CLG, O_CLB = \
            0, 32, 40, 44, 45, 46, 47, 48, 48 + 496, 48 + 512, 48 + 528
        scale = 1.0 / math.sqrt(192.0)
        pend = []

        def defer_stat(f, delay=1):
            pend.append([delay, f])

        def tick_stat():
            for e in pend:
                e[0] -= 1
            while pend and pend[0][0] <= 0:
                pend.pop(0)[1]()

        def flush_stat():
            while pend:
                pend.pop(0)[1]()

        def phase0(l, lp, xsrc):
            t0 = lp * TH
            for rnd in range(2):
                for g4 in range(8):
                    so = (g4 % 2) * 4
                    st = stage[:, so:so + 4, :]
                    Bst = B_stage[so:so + 4]
                    src = xsrc[g4 * 512:(g4 + 1) * 512, t0:t0 + TH].rearrange("(c q) t -> q c t", q=128)
                    S.dma("sp", st, src, reads=[B_x[lp][g4 * 4 + c] for c in range(4)], writes=Bst)
                    for c8 in range(4):
                        c = g4 * 4 + c8
                        if rnd == 0:
                            k = sqbuf()
                            act(sqt[k][:, :], st[:, c8, :], AF.Square, [Bst[c8]], [B_sqt[k]])
                            mm(ps[2][:, :], ones[:, :], sqt[k][:, :], c == 0, c == 31,
                               [B_sqt[k], B["cst"]], [B_ps[2]])
                        else:
                            stt(hT[:, c, :], st[:, c8, :], par[:, O_LNG + c:O_LNG + c + 1], rbc[:, :],
                                ALU.mult, ALU.mult, [Bst[c8], B["par"], B["rbc"]], [B_hT[c]])
                if rnd == 0:
                    rstd_from(rbc[:, :], B["rbc"], ps[2][:, :], [B_ps[2]], float(D))
            S.dma("sp", hs[lp * D:(lp + 1) * D, :].rearrange("(c q) t -> q c t", q=128), hT[:, :, :],
                  reads=B_hT, writes=[B_hs[lp]])

        def block_mm(W, Bw, bank, M, rhs_of, rhsB_of, inc_last=True, cols=slice(0, TH)):
            for c in range(32):
                mm(ps[bank][0:M, cols], W[:, c, 0:M], rhs_of(c), c == 0, c == 31,
                   [Bw, rhsB_of(c)], [B_ps[bank]], inc=(c == 31 and inc_last) or (c == 31))

        def phaseA(l, lp, jlist):
            t0 = lp * TH
            for bi, j in enumerate(jlist):
                W, Bw = wtake(("in", l, j))
                if pcoll and bi % 6 == 2:
                    pcoll.pop(0)()
                a_bank = wstate["prev"]
                bank = abank()
                wstate["prev"] = bank
                M = 64 if j == 12 else 128
                block_mm(W, Bw, bank, M, lambda c: hT[:, c, :], lambda c: B_hT[c])
                is_conv = 13 <= j < 45
                if is_conv and lp == 0:
                    hb = 4 if (j - 13) % 2 == 0 else 5
                    block_mm(W, Bw, hb, 128, lambda c: hTh[:, c, :], lambda c: B["hTh"], cols=slice(0, HALO))
                tick_stat()
                P = ps[bank]
                BP = B_ps[bank]
                if j < 12:
                    isq = j < 8
                    ci = j if isq else j - 8
                    if isq:
                        stv = [stage[:, c_, :] for c_ in range(8)]
                        Bst = B_stage
                    else:
                        stv = [acc[0][:, :], acc[1][:, :], sig[0][:, :], sig[1][:, :]]
                        Bst = [B_acc[0], B_acc[1], B_sig[0], B_sig[1]]
                    nlast = 7 if isq else 3
                    acopy(stv[ci], P[:, :], [BP], [Bst[ci]])
                    k = sqbuf()
                    act(sqt[k][:, :], P[:, :], AF.Square, [BP], [B_sqt[k]])

                    def _stat(k=k, ci=ci, nlast=nlast):
                        mm(ps[2][:, :], ones[:, :], sqt[k][:, :], ci == 0, ci == nlast,
                           [B_sqt[k], B["cst"]], [B_ps[2]])
                    defer_stat(_stat, 1)
                    if ci == nlast:
                        flush_stat()
                        n = 1024.0 if isq else 512.0
                        rstd_from(rbc2[:, :], B["rbc2"], ps[2][:, :], [B_ps[2]], n)
                        dstT, Bd, og = (qcn, B_qcn, O_QA) if isq else (kvn, B_kvn, O_KVA)
                        for c2 in range(nlast + 1):
                            stt(dstT[:, c2, :], stv[c2], par[:, og + c2:og + c2 + 1], rbc2[:, :],
                                ALU.mult, ALU.mult, [Bst[c2], B["par"], B["rbc2"]], [Bd[c2]])
                        if isq:
                            S.dma("sp", qs[lp * 1024:(lp + 1) * 1024, :].rearrange("(c q) t -> q c t", q=128),
                                  qcn[:, :, :], reads=B_qcn, writes=[B_qs[lp]])
                elif j == 12:
                    acopy(kpe_f[:, :], P[0:64, :], [BP], [B["kpe_f"]])
                    k = sqbuf()
                    act(sqt[k][0:64, :], P[0:64, :], AF.Square, [BP], [B_sqt[k]])
                    mm(ps[2][:, :], ones[0:64, :], sqt[k][0:64, :], True, True, [B_sqt[k], B["cst"]], [B_ps[2]])
                    acopy(spe[:, :], ps[2][:, :], [B_ps[2]], [B["spe"]])
                    ts(kpe_f[:, :], kpe_f[:, :], par[0:64, O_KNB:O_KNB + 1], None, ALU.mult, None,
                       [B["kpe_f"], B["par"]], [B["kpe_f"]])
                    rope(kpe_rot[:, :], [B["kpe_rot"]], kpe_f, [B["kpe_f"]], t0)
                elif j < 45:
                    jj = (j - 13) // 2
                    if (j - 13) % 2 == 1:
                        ub = jj % 2
                        act(sig[ub][:, :], P[:, :], AF.Sigmoid, [BP], [B_sig[ub]])
                        if lp == 0:
                            act(sgh[:, :], ps[5][:, 0:HALO], AF.Sigmoid, [B_ps[5]], [B["sgh"]])
                            tt(uh[:, :], ps[4][:, 0:HALO], sgh[:, :], ALU.mult, [B_ps[4], B["sgh"]], [B["uh"]])
                            ts(ubuf[ub][:, 0:30], uh[:, 2:HALO], flg_t[:, 1:2], None, ALU.mult, None,
                               [B["uh"], B["cst"]], [B_ubuf[ub]])
                        else:
                            cp(ubuf[ub][:, 0:30], halo[:, jj, :], [B_halo[jj]], [B_ubuf[ub]])
                        tt(ubuf[ub][:, 30:30 + TH], ps[a_bank][:, :], sig[ub][:, :], ALU.mult,
                           [B_ps[a_bank], B_sig[ub]], [B_ubuf[ub]])
                        if lp < NPL - 1:
                            cp(halo[:, jj, :], ubuf[ub][:, TH:TH + 30], [B_ubuf[ub]], [B_halo[jj]])
                        wo = O_WDW + jj * 31
                        ts(acc[ub][:, :], ubuf[ub][:, 0:TH], par[:, wo:wo + 1],
                           par[:, O_BDW + jj:O_BDW + jj + 1], ALU.mult, ALU.add,
                           [B_ubuf[ub], B["par"]], [B_acc[ub]])
                        for tap in range(1, 31):
                            last = tap == 30
                            o_ap = catT[:, 16 + jj, :] if last else acc[ub][:, :]
                            stt(o_ap, ubuf[ub][:, tap:tap + TH], par[:, wo + tap:wo + tap + 1], acc[ub][:, :],
                                ALU.mult, ALU.add, [B_ubuf[ub], B["par"], B_acc[ub]],
                                [B_cat[16 + jj]] if last else [B_acc[ub]], nosync=(tap > 1))
                        kk = {}

                        def _sq(jj=jj, kk=kk):
                            kk["k"] = sqbuf()
                            act(sqt[kk["k"]][:, :], catT[:, 16 + jj, :], AF.Square, [B_cat[16 + jj]], [B_sqt[kk["k"]]])

                        def _stat(jj=jj, kk=kk):
                            k = kk["k"]
                            mm(ps[2][:, :], ones[:, :], catT[:, 16 + jj, :], jj == 0, jj == 15,
                               [B_cat[16 + jj], B["cst"]], [B_ps[2]])
                            mm(ps[3][:, :], ones[:, :], sqt[k][:, :], jj == 0, jj == 15,
                               [B_sqt[k], B["cst"]], [B_ps[3]])
                        defer_stat(_sq, 3)
                        defer_stat(_stat, 4)
                        if jj == 15:
                            flush_stat()
                            ts(mu_bc[:, :], ps[2][:, :], 1.0 / 2048, None, ALU.mult, None, [B_ps[2]], [B["mu"]])
                            tt(rs_bc[:, :], mu_bc[:, :], mu_bc[:, :], ALU.mult, [B["mu"]], [B["rs"]])
                            stt(rs_bc[:, :], ps[3][:, :], 1.0 / 2048, rs_bc[:, :], ALU.mult, ALU.subtract,
                                [B_ps[3], B["rs"]], [B["rs"]])
                            rstd_from(rs_bc[:, :], B["rs"], rs_bc[:, :], [B["rs"]], 1.0)
                elif j < 61:
                    h = j - 45
                    act(catT[:, h, :], P[:, :], AF.Silu, [BP], [B_cat[h]])
                else:
                    jj = j - 61
                    tb = jj % 2
                    act(sig[tb][:, :], P[:, :], AF.Silu, [BP], [B_sig[tb]])
                    tt(tmpf[tb][:, :], catT[:, 16 + jj, :], mu_bc[:, :], ALU.subtract,
                       [B_cat[16 + jj], B["mu"]], [B_tmpf[tb]])
                    tt(tmpf[tb][:, :], tmpf[tb][:, :], rs_bc[:, :], ALU.mult, [B_tmpf[tb], B["rs"]], [B_tmpf[tb]])
                    act(tmpf[tb][:, :], tmpf[tb][:, :], AF.Silu, [B_tmpf[tb], B["par"]], [B_tmpf[tb]],
                        bias=par[:, O_CLB + jj:O_CLB + jj + 1], scale=par[:, O_CLG + jj:O_CLG + jj + 1])
                    tt(catT[:, 16 + jj, :], tmpf[tb][:, :], sig[tb][:, :], ALU.mult,
                       [B_tmpf[tb], B_sig[tb]], [B_cat[16 + jj]])
            flush_stat()

        def kv_stage(l, lp):
            t0 = lp * TH
            rb = [rbc2, rbc]
            Brb = [B["rbc2"], B["rbc"]]
            sbank = [3, 2]

            def load_wk(h):
                r0 = (l * NH + h) * 128
                S.dma("pool", wk[h % 2][:, :, :], wk_r[r0:r0 + 128, :].rearrange("p (c n) -> p c n", c=4),
                      writes=[B_wk[h % 2]])

            def v_group(g):
                r0 = (l * 4 + g) * 128
                S.dma("pool", wv[:, :, :], wv_r[r0:r0 + 128, :].rearrange("p (c n) -> p c n", c=4),
                      writes=[B_wv])
                for tt_ in range(4):
                    bank = 6 + tt_ % 2
                    for c in range(4):
                        mm(ps[bank][:, :], kvn[:, c, tt_ * 128:(tt_ + 1) * 128], wv[:, c, :], c == 0, c == 3,
                           [B_kvn[c], B_wv], [B_ps[bank]], inc=(c == 3))
                    acopy(vtmp[:, tt_, :], ps[bank][:, :], [B_ps[bank]], [Bv_vtmp[tt_]])
                for k2 in range(2):
                    pr = 2 * g + k2
                    S.dma("sp", vcp_rows(pr, t0, t0 + TH).rearrange("(t q) d -> q t d", q=128),
                          vtmp[:, :, k2 * 256:(k2 + 1) * 256], reads=Bv_vtmp, writes=[B_vc[pr]])

            KB = [0, 1, 4, 5]

            def k_mm(h):
                WK = wk[h % 2]
                bk = KB[h % 4]
                for c in range(4):
                    mm(ps[bk][:, :], WK[:, c, :], kvn[:, c, :], c == 0, c == 3,
                       [B_wk[h % 2], B_kvn[c]], [B_ps[bk]], inc=(c == 3))
                k = sqbuf()
                act(sqt[k][:, :], ps[bk][:, :], AF.Square, [B_ps[bk]], [B_sqt[k]])
                return bk, k

            def k_chain1(h, bk, k):
                i2 = h % 2
                sbk = sbank[i2]
                mm(ps[sbk][:, :], ones[:, :], sqt[k][:, :], True, True, [B_sqt[k], B["cst"]], [B_ps[sbk]])
                tt(rb[i2][:, :], ps[sbk][:, :], spe[:, :], ALU.add, [B_ps[sbk], B["spe"]], [Brb[i2]])
                rstd_from(rb[i2][:, :], Brb[i2], rb[i2][:, :], [Brb[i2]], 192.0)

            def k_chain2(h, bk, k):
                i2 = h % 2
                i4 = h % 4
                stt(ktmp_a[i4], ps[bk][:, :], par[:, O_KNA:O_KNA + 1], rb[i2][:, :], ALU.mult, ALU.mult,
                    [B_ps[bk], B["par"], Brb[i2]], [B_hT[4 + i4]])
                tt(ktmp_b[i4], kpe_rot[:, :], rb[i2][0:64, :], ALU.mult,
                   [B["kpe_rot"], Brb[i2]], [B_hT[8 + i4]])
                S.dma("sp", kc_a_rows(h)[:, t0:t0 + TH], ktmp_a[i4], reads=[B_hT[4 + i4]], writes=[B_kc[h]])
                S.dma("sp", kc_b[h * 64:(h + 1) * 64, t0:t0 + TH], ktmp_b[i4], reads=[B_hT[8 + i4]], writes=[B_kc[h]])

            load_wk(0)
            load_wk(1)
            cur = k_mm(0)
            k_chain1(0, *cur)
            for h in range(NH):
                if h % 4 == 0:
                    v_group(h // 4)
                nxt = k_mm(h + 1) if h + 1 < NH else None
                if nxt is not None:
                    k_chain1(h + 1, *nxt)
                k_chain2(h, *cur)
                if h + 2 < NH:
                    load_wk(h + 2)
                cur = nxt

        pcoll = []

        def exchange(l):
            S.coll(xh_t.ap().opt(), xh_gt.ap().opt(), groups, reads=[B["xh"]], writes=[B["xh_g"]])
            for i in range(2):
                pcoll.append(lambda i=i: S.coll(kc_a_t[i].ap().opt(), kc_a_gt[i].ap().opt(), groups,
                                                reads=B_kc, writes=[B["kc_g"]]))
            pcoll.append(lambda: S.coll(kc_b_t.ap().opt(), kc_b_gt.ap().opt(), groups, reads=B_kc, writes=[B["kc_g"]]))
            for i in range(2):
                pcoll.append(lambda i=i: S.coll(vcp_t[i].ap().opt(), vcp_gt[i].ap().opt(), groups,
                                                reads=B_vc, writes=[B["vc_g"]]))

        def flush_coll():
            while pcoll:
                pcoll.pop(0)()

        def halo_prep(l):
            xhs = stage[:, 0:2, :].rearrange("p a (c t) -> p (a c) t", t=HALO)
            Bst = B_stage[0:2]
            S.dma("sp", xhs, xh_g[0:D, :].rearrange("(c q) t -> q c t", q=128), reads=[B["xh_g"]], writes=Bst)
            sqh = sqt[0][:, :].rearrange("p (c t) -> p c t", t=HALO)
            for half in range(2):
                act(sqh, xhs[:, half * 16:(half + 1) * 16, :], AF.Square, Bst, [B_sqt[0]])
                for c in range(16):
                    cc = half * 16 + c
                    mm(ps[2][:, 0:HALO], ones[:, :], sqh[:, c, :], cc == 0, cc == 31, [B_sqt[0], B["cst"]], [B_ps[2]])
            rstd_from(rh[:, :], B["rh"], ps[2][:, 0:HALO], [B_ps[2]], float(D))
            for c in range(32):
                stt(hTh[:, c, :], xhs[:, c, :], par[:, O_LNG + c:O_LNG + c + 1], rh[:, :],
                    ALU.mult, ALU.mult, Bst + [B["par"], B["rh"]], [B["hTh"]])

        def attention(l, lp):
            t0 = lp * TH
            nown = (t0 // 128) + 4
            nkt = 8 + nown

            def load_wq(h):
                r0 = (l * NH + h) * 128
                S.dma("pool", wq[h % 2][:, :, :], wq_r[r0:r0 + 128, :].rearrange("p (c n) -> p c n", c=8),
                      writes=[B_wq[h % 2]])

            def prepA(h):
                i2 = h % 2
                S.dma("sp", ktw_a[i2][:, 0:TOK], kc_a_g_rows(h), reads=[B["kc_g"]], writes=Bv_kta[i2])
                S.dma("sp", ktw_a[i2][:, TOK:TOK + t0 + TH], kc_a_rows(h)[:, 0:t0 + TH],
                      reads=[B_kc[h]], writes=Bv_kta[i2])
                S.dma("sp", ktw_b[i2][:, 0:TOK], kc_b_g[h * 64:(h + 1) * 64, :], reads=[B["kc_g"]], writes=Bv_ktb[i2])
                S.dma("sp", ktw_b[i2][:, TOK:TOK + t0 + TH], kc_b[h * 64:(h + 1) * 64, 0:t0 + TH],
                      reads=[B_kc[h]], writes=Bv_ktb[i2])
                if h % 2 == 0:
                    pr = h // 2
                    vb = pr % 2
                    S.dma("sp", vgw[vb][:, 0:8, :], vcp_g_rows(pr).rearrange("(t q) d -> q t d", q=128),
                          reads=[B["vc_g"]], writes=Bv_vgw[vb])
                    S.dma("sp", vgw[vb][:, 8:8 + nown, :],
                          vcp_rows(pr, 0, t0 + TH).rearrange("(t q) d -> q t d", q=128),
                          reads=[B_vc[pr]], writes=Bv_vgw[vb])
                WQ = wq[i2]
                bqa = mainbank()
                for c in range(8):
                    mm(ps[bqa][:, :], WQ[:, c, 0:128], qcn[:, c, :], c == 0, c == 7,
                       [B_wq[i2], B_qcn[c]], [B_ps[bqa]], inc=(c == 7))
                bqb = mainbank()
                for c in range(8):
                    mm(ps[bqb][0:64, :], WQ[:, c, 128:192], qcn[:, c, :], c == 0, c == 7,
                       [B_wq[i2], B_qcn[c]], [B_ps[bqb]], inc=(c == 7))
                act(sqt[0][:, :], ps[bqa][:, :], AF.Square, [B_ps[bqa]], [B_sqt[0]])
                act(sqt[1][0:64, :], ps[bqb][0:64, :], AF.Square, [B_ps[bqb]], [B_sqt[1]])
                return bqa, bqb

            def prepB(h, bqa, bqb):
                i2 = h % 2
                mm(ps[2][:, :], ones[:, :], sqt[0][:, :], True, False, [B_sqt[0], B["cst"]], [B_ps[2]], inc=False)
                mm(ps[2][:, :], ones[0:64, :], sqt[1][0:64, :], False, True, [B_sqt[1], B["cst"]], [B_ps[2]])
                rstd_from(rbc[:, :], B["rbc"], ps[2][:, :], [B_ps[2]], 192.0)
                stt(qt_a[i2][:, :], ps[bqa][:, :], par[:, O_QNA:O_QNA + 1], rbc[:, :], ALU.mult, ALU.mult,
                    [B_ps[bqa], B["par"], B["rbc"]], [B_qta[i2]])
                stt(qb_f[:, :], ps[bqb][0:64, :], par[0:64, O_QNB:O_QNB + 1], rbc[0:64, :], ALU.mult, ALU.mult,
                    [B_ps[bqb], B["par"], B["rbc"]], [B["qb_f"]])
                rope(qt_b[i2][:, :], [B_qtb[i2]], qb_f, [B["qb_f"]], t0)

            def tile_geo(kt):
                jd = kt - 8 - t0 // 128
                q0 = jd * 128 if jd > 0 else 0
                return jd, q0, (4, 5, 3)[kt % 3], kt % 3

            def score_mm(h, kt):
                i2 = h % 2
                jd, q0, sb_, pi_ = tile_geo(kt)
                mm(ps[sb_][:, q0:TH], ktw_a[i2][:, kt * 128:(kt + 1) * 128], qt_a[i2][:, q0:TH], True, False,
                   Bv_kta[i2] + [B_qta[i2]], [B_ps[sb_]], inc=False)
                mm(ps[sb_][:, q0:TH], ktw_b[i2][:, kt * 128:(kt + 1) * 128], qt_b[i2][:, q0:TH], False, True,
                   Bv_ktb[i2] + [B_qtb[i2]], [B_ps[sb_]])

            def score_rest(h, kt):
                i2 = h % 2
                vb = (h // 2) % 2
                vs = (h % 2) * 128
                own = kt >= 8
                jd, q0, sb_, pi_ = tile_geo(kt)
                bias = -8.0 if own else flg_t[:, 0:1]
                act(pT[pi_][:, q0:TH], ps[sb_][:, q0:TH], AF.Exp, [B_ps[sb_], B["cst"]], [B_pT[pi_]],
                    bias=bias, scale=scale)
                if jd >= 0:
                    tt(pT[pi_][:, q0:q0 + 128], pT[pi_][:, q0:q0 + 128], tri[:, :], ALU.mult,
                       [B_pT[pi_], B["cst"]], [B_pT[pi_]])
                mm(ps[6][:, q0:TH], vgw[vb][:, kt, vs:vs + 128], pT[pi_][:, q0:TH],
                   kt == 0, kt == nkt - 1, Bv_vgw[vb] + [B_pT[pi_]], [B_ps[6]], inc=(kt == nkt - 1))
                mm(ps[7][:, q0:TH], ones[:, :], pT[pi_][:, q0:TH],
                   kt == 0, kt == nkt - 1, [B["cst"], B_pT[pi_]], [B_ps[7]])

            def finish(h):
                tb = h % 2
                act(tmpf[tb][:, :], ps[7][:, :], AF.Ln, [B_ps[7]], [B_tmpf[tb]])
                act(tmpf[tb][:, :], tmpf[tb][:, :], AF.Exp, [B_tmpf[tb]], [B_tmpf[tb]], scale=-1.0)
                tt(tmpf[tb][:, :], ps[6][:, :], tmpf[tb][:, :], ALU.mult, [B_ps[6], B_tmpf[tb]], [B_tmpf[tb]])
                tt(catT[:, h, :], tmpf[tb][:, :], catT[:, h, :], ALU.mult, [B_tmpf[tb], B_cat[h]], [B_cat[h]])

            load_wq(0)
            load_wq(1)
            ba, bb = prepA(0)
            prepB(0, ba, bb)
            for h in range(NH):
                nxt = None
                if h + 1 < NH:
                    nxt = prepA(h + 1)
                score_mm(h, 0)
                score_mm(h, 1)
                for kt in range(nkt):
                    if kt + 2 < nkt:
                        score_mm(h, kt + 2)
                    score_rest(h, kt)
                    if kt == 2 and nxt is not None:
                        prepB(h + 1, *nxt)
                        if h + 2 < NH:
                            load_wq(h + 2)
                finish(h)

        def phaseC(l, lp, xsrc):
            t0 = lp * TH
            for jo in range(32):
                W, Bw = wtake(("out", l, jo))
                bank = abank()
                xi = jo % 3
                S.dma("sp", xin[xi], xsrc[jo * 128:(jo + 1) * 128, t0:t0 + TH],
                      reads=[B_x[lp][jo]], writes=[B_xin[xi]])
                block_mm(W, Bw, bank, 128, lambda c: catT[:, c, :], lambda c: B_cat[c])
                tt(xin[xi], ps[bank][:, :], xin[xi], ALU.add, [B_ps[bank], B_xin[xi]], [B_xin[xi]])
                S.dma("sp", out[jo * 128:(jo + 1) * 128, t0:t0 + TH], xin[xi],
                      reads=[B_xin[xi]], writes=[B_x[lp][jo]])

        wprefetch(NW - 1)
        for l in range(NL):
            xsrc = xT if l == 0 else out
            S.dma("sp", par[:, :], sp_r[l * 128:(l + 1) * 128, :], writes=[B["par"]])
            for lp in range(NPL):
                phase0(l, lp, xsrc)
                if lp == NPL - 1:
                    for q4 in range(4):
                        S.dma("sp", xh[q4 * 1024:(q4 + 1) * 1024, :], xsrc[q4 * 1024:(q4 + 1) * 1024, TOK - HALO:TOK],
                              reads=B_x[lp][q4 * 8:(q4 + 1) * 8], writes=[B["xh"]])
                phaseA(l, lp, A1_ORDER)
                kv_stage(l, lp)
            exchange(l)
            for lp in range(NPL):
                S.dma("sp", hT[:, :, :], hs[lp * D:(lp + 1) * D, :].rearrange("(c q) t -> q c t", q=128),
                      reads=[B_hs[lp]], writes=B_hT)
                S.dma("sp", qcn[:, :, :], qs[lp * 1024:(lp + 1) * 1024, :].rearrange("(c q) t -> q c t", q=128),
                      reads=[B_qs[lp]], writes=B_qcn)
                if lp == 0:
                    halo_prep(l)
                phaseA(l, lp, A2_ORDER)
                flush_coll()
                attention(l, lp)
                phaseC(l, lp, xsrc)
        for p in range(NPL):
            S.wait_bufs("sp", B_x[p])
        print("program built: insts=%d, sems pe=%d act=%d dve=%d" % (S.n_inst, S.cnt["pe"], S.cnt["act"], S.cnt["dve"]))
    return nc


def _in_col_order():
    blocks = []
    blocks += [np.arange(i * 128, (i + 1) * 128) for i in range(8)]
    blocks += [np.arange(1024 + i * 128, 1024 + (i + 1) * 128) for i in range(4)]
    kpe = np.full(128, -1)
    kpe[:64] = np.arange(1536, 1600)
    blocks.append(kpe)
    for j in range(16):
        blocks.append(np.arange(3648 + j * 128, 3648 + (j + 1) * 128))
        blocks.append(np.arange(3648 + 2048 + j * 128, 3648 + 2048 + (j + 1) * 128))
    blocks += [np.arange(1600 + h * 128, 1600 + (h + 1) * 128) for h in range(16)]
    blocks += [np.arange(7744 + j * 128, 7744 + (j + 1) * 128) for j in range(16)]
    return blocks


def _block_layout(w, blocks):
    K = w.shape[0]
    kc = K // 128
    outp = np.zeros((len(blocks), 128, kc, 128), np.float32)
    w3 = w.reshape(kc, 128, w.shape[1])
    for j, cols in enumerate(blocks):
        valid = cols >= 0
        sub = w3[:, :, cols[valid]]
        outp[j, :, :, :valid.sum()] = sub.transpose(1, 0, 2)
    return outp.reshape(len(blocks) * 128, kc * 128)


def prepare_weights(NL, ln_g, w_in, q_a_norm, w_q_up, kv_a_norm, w_kv_up, q_norm, k_norm,
                    w_dw, b_dw, conv_ln_g, conv_ln_b, w_out):
    blocks_in = _in_col_order()
    blocks_out = [np.arange(j * 128, (j + 1) * 128) for j in range(32)]
    w_in_r = np.concatenate([_block_layout(np.asarray(w_in[l]), blocks_in) for l in range(NL)], 0)
    w_out_r = np.concatenate([_block_layout(np.asarray(w_out[l]), blocks_out) for l in range(NL)], 0)
    wq_l, wk_l, wv_l, sp_l = [], [], [], []
    for l in range(NL):
        wq = np.asarray(w_q_up[l]).reshape(8, 128, NH, 192)
        wq_l.append(wq.transpose(2, 1, 0, 3).reshape(NH * 128, 8 * 192))
        wkv = np.asarray(w_kv_up[l]).reshape(4, 128, NH, 256)
        wk_l.append(wkv[:, :, :, :128].transpose(2, 1, 0, 3).reshape(NH * 128, 4 * 128))
        wv = wkv[:, :, :, 128:].reshape(4, 128, 4, 4, 128)
        wv_l.append(wv.transpose(2, 1, 0, 3, 4).reshape(4 * 128, 4 * 512))
        sp = np.zeros((128, NSP), np.float32)
        sp[:, 0:32] = np.asarray(ln_g[l]).reshape(32, 128).T
        sp[:, 32:40] = np.asarray(q_a_norm[l]).reshape(8, 128).T
        sp[:, 40:44] = np.asarray(kv_a_norm[l]).reshape(4, 128).T
        sp[:, 44] = np.asarray(q_norm[l])[:128]
        sp[:64, 45] = np.asarray(q_norm[l])[128:]
        sp[:, 46] = np.asarray(k_norm[l])[:128]
        sp[:64, 47] = np.asarray(k_norm[l])[128:]
        sp[:, 48:48 + 496] = np.asarray(w_dw[l]).reshape(31, 16, 128).transpose(2, 1, 0).reshape(128, 496)
        sp[:, 544:560] = np.asarray(b_dw[l]).reshape(16, 128).T
        sp[:, 560:576] = np.asarray(conv_ln_g[l]).reshape(16, 128).T
        sp[:, 576:592] = np.asarray(conv_ln_b[l]).reshape(16, 128).T
        sp_l.append(sp)
    cst = np.zeros((128, 130), np.float32)
    half = 32
    inv_freq = (10000.0 ** (-np.arange(half, dtype=np.float32) / half)).astype(np.float32)
    cst[:64, 0] = np.concatenate([inv_freq, inv_freq])
    cst[:32, 1] = 1.0
    cst[32:64, 1] = -1.0
    kk = np.arange(128)
    cst[:, 2:130] = (kk[None, :] >= kk[:, None]).astype(np.float32)
    return dict(
        w_in_r=np.ascontiguousarray(w_in_r), w_out_r=np.ascontiguousarray(w_out_r),
        wq_r=np.ascontiguousarray(np.concatenate(wq_l, 0)), wk_r=np.ascontiguousarray(np.concatenate(wk_l, 0)),
        wv_r=np.ascontiguousarray(np.concatenate(wv_l, 0)), sp_r=np.ascontiguousarray(np.concatenate(sp_l, 0)),
        cst=cst)


def run(NL, n_cores, x, positions, wd):
    nc = build_program(NL, n_cores)
    in_maps = []
    for c in range(n_cores):
        b, half = c // 2, c % 2
        m = dict(wd)
        m["xT"] = np.ascontiguousarray(np.asarray(x[b])[half * TOK:(half + 1) * TOK].T)
        m["pos"] = np.ascontiguousarray(np.asarray(positions[b])[half * TOK:(half + 1) * TOK].reshape(1, TOK).astype(np.int32))
        f = np.zeros((128, 2), np.float32)
        f[:, 0] = -8.0 if half == 1 else -30000.0
        f[:, 1] = 1.0 if half == 1 else 0.0
        m["flg"] = f
        in_maps.append(m)
    res = run_bass_kernel_spmd(nc, in_maps, core_ids=list(range(n_cores)))
    y = np.zeros((n_cores // 2, SEQ, D), np.float32)
    for c in range(n_cores):
        b, half = c // 2, c % 2
        y[b, half * TOK:(half + 1) * TOK] = res.results[c]["out"].T
    return y


def kernel(x, positions, ln_g, w_in, q_a_norm, w_q_up, kv_a_norm, w_kv_up, q_norm, k_norm,
           w_dw, b_dw, conv_ln_g, conv_ln_b, w_out):
    wd = prepare_weights(DEPTH, ln_g, w_in, q_a_norm, w_q_up, kv_a_norm, w_kv_up, q_norm, k_norm,
                         w_dw, b_dw, conv_ln_g, conv_ln_b, w_out)
    y = run(DEPTH, N_CORES, np.asarray(x), np.asarray(positions), wd)
    return y.astype(np.float32)
```

```python
from contextlib import ExitStack
import math
import numpy as np
import concourse.bass as bass
import concourse.mybir as mybir
from concourse.bass_utils import run_bass_kernel_spmd

F32 = mybir.dt.float32
BF16 = mybir.dt.bfloat16
I32 = mybir.dt.int32
ALU = mybir.AluOpType
AF = mybir.ActivationFunctionType

D = 4096
SEQ = 2048
DEPTH = 4
TOK = 1024
TH = 512
NPL = TOK // TH
NH = 16
EPS = 1e-6
NB_IN = 77
NB_A1 = 13
NSP = 592
SAME_ENGINE_SYNC = True
N_DMA_SEMS = 24
N_CORES = 8
HALO = 32


class Buf:
    __slots__ = ("name", "w", "rs")

    def __init__(self, name):
        self.name = name
        self.w = None
        self.rs = []


class Sched:
    def __init__(self, nc, es):
        self.nc = nc
        self.eng = {"pe": nc.tensor, "act": nc.scalar, "dve": nc.vector,
                    "pool": nc.gpsimd, "sp": nc.sync}
        self.sem = {}
        self.cnt = {}
        for k in self.eng:
            self.sem[k] = es.enter_context(nc.semaphore("s_" + k))
            self.cnt[k] = 0
        self.dsem = [es.enter_context(nc.semaphore("d%d" % i)) for i in range(N_DMA_SEMS)]
        self.dcnt = [0] * N_DMA_SEMS
        self.dnext = 0
        self.csem = es.enter_context(nc.semaphore("s_coll"))
        self.ccnt = 0
        self.seen = {k: {} for k in self.eng}
        self.n_inst = 0
        self.nosync = False

    def _semof(self, key):
        if isinstance(key, tuple):
            return self.dsem[key[1]]
        if key == "coll":
            return self.csem
        return self.sem[key]

    def _wait(self, E, tok):
        key, val = tok
        if key == E and (E == "pe" or not SAME_ENGINE_SYNC or self.nosync):
            return
        if self.seen[E].get(key, 0) >= val:
            return
        self.seen[E][key] = val
        self.eng[E].wait_ge(self._semof(key), val)
        self.n_inst += 1

    def _deps(self, E, reads, writes):
        for b in reads:
            if b.w is not None:
                self._wait(E, b.w)
        for b in writes:
            if b.w is not None:
                self._wait(E, b.w)
            for t in b.rs:
                self._wait(E, t)

    def _record(self, tok, reads, writes):
        for b in reads:
            b.rs.append(tok)
        for b in writes:
            b.w = tok
            b.rs = []

    def op(self, E, fn, reads=(), writes=(), inc=True, nosync=False):
        self.nosync = nosync
        self._deps(E, reads, writes)
        self.nosync = False
        inst = fn(self.eng[E])
        tok = (E, self.cnt[E] + 1)
        if inc:
            inst.then_inc(self.sem[E], 1)
            self.cnt[E] += 1
        self._record(tok, reads, writes)
        self.n_inst += 1
        return tok

    def dma(self, Q, out, in_, reads=(), writes=()):
        self._deps(Q, reads, writes)
        i = self.dnext
        self.dnext = (self.dnext + 1) % N_DMA_SEMS
        key = ("d", i)
        if self.dcnt[i] > 0:
            self._wait(Q, (key, self.dcnt[i]))
        inst = self.eng[Q].dma_start(out=out, in_=in_)
        self.dcnt[i] += 16
        inst.then_inc(self.dsem[i], 16)
        tok = (key, self.dcnt[i])
        self._record(tok, reads, writes)
        self.n_inst += 1
        return tok

    def coll(self, in_ap, out_ap, groups, reads, writes):
        self._deps("pool", reads, writes)
        inst = self.nc.gpsimd.collective_compute("AllGather", ALU.bypass, replica_groups=groups,
                                                 ins=[in_ap], outs=[out_ap])
        inst.then_inc(self.csem)
        self.ccnt += 1
        tok = ("coll", self.ccnt)
        self._record(tok, reads, writes)
        self.n_inst += 1
        return tok

    def wait_bufs(self, E, bufs):
        for b in bufs:
            if b.w is not None:
                self._wait(E, b.w)
            for t in b.rs:
                self._wait(E, t)


def build_program(NL, n_cores=N_CORES):
    nc = bass.Bass("TRN2", target_bir_lowering=False)
    dt_in = lambda name, shape, dt=F32: nc.dram_tensor(name, shape, dt, kind="ExternalInput").ap()
    xT = dt_in("xT", [D, TOK])
    pos = dt_in("pos", [1, TOK], I32)
    w_in_r = dt_in("w_in_r", [NL * NB_IN * 128, 4096])
    w_out_r = dt_in("w_out_r", [NL * 32 * 128, 4096])
    wq_r = dt_in("wq_r", [NL * NH * 128, 8 * 192])
    wk_r = dt_in("wk_r", [NL * NH * 128, 4 * 128])
    wv_r = dt_in("wv_r", [NL * 4 * 128, 4 * 512])
    sp_r = dt_in("sp_r", [NL * 128, NSP])
    cst = dt_in("cst", [128, 130])
    flg = dt_in("flg", [128, 2])
    out = nc.dram_tensor("out", [D, TOK], F32, kind="ExternalOutput").ap()
    kc_a_t = [nc.dram_tensor("kc_a%d" % i, [8 * 128, TOK], BF16) for i in range(2)]
    kc_b_t = nc.dram_tensor("kc_b", [NH * 64, TOK], BF16)
    vcp_t = [nc.dram_tensor("vcp%d" % i, [4 * TOK, 256], BF16) for i in range(2)]
    xh_t = nc.dram_tensor("xh", [D, HALO], F32)
    kc_a_gt = [nc.dram_tensor("kc_a_g%d" % i, [2 * 8 * 128, TOK], BF16) for i in range(2)]
    kc_b_gt = nc.dram_tensor("kc_b_g", [2 * NH * 64, TOK], BF16)
    vcp_gt = [nc.dram_tensor("vcp_g%d" % i, [2 * 4 * TOK, 256], BF16) for i in range(2)]
    xh_gt = nc.dram_tensor("xh_g", [2 * D, HALO], F32)
    kc_b, xh = kc_b_t.ap(), xh_t.ap()
    kc_b_g, xh_g = kc_b_gt.ap(), xh_gt.ap()

    def kc_a_rows(h):
        return kc_a_t[h // 8].ap()[(h % 8) * 128:(h % 8 + 1) * 128, :]

    def kc_a_g_rows(h):
        return kc_a_gt[h // 8].ap()[(h % 8) * 128:(h % 8 + 1) * 128, :]

    def vcp_rows(pr, r0, r1):
        return vcp_t[pr // 4].ap()[(pr % 4) * TOK + r0:(pr % 4) * TOK + r1, :]

    def vcp_g_rows(pr):
        return vcp_gt[pr // 4].ap()[(pr % 4) * TOK:(pr % 4 + 1) * TOK, :]
    hs = nc.dram_tensor("hs", [NPL * D, TH], BF16).ap()
    qs = nc.dram_tensor("qs", [NPL * 1024, TH], BF16).ap()
    groups = [[2 * i, 2 * i + 1] for i in range(n_cores // 2)]

    with ExitStack() as es:
        S = Sched(nc, es)

        def sb(name, shape, dt):
            return es.enter_context(nc.sbuf_tensor(name, shape, dt))

        hT = sb("hT", [128, 32, TH], BF16)
        catT = sb("catT", [128, 32, TH], BF16)
        NW = 4
        wbuf = [sb("wbuf%d" % i, [128, 4, 1024], BF16) for i in range(NW)]
        wq = [sb("wq%d" % i, [128, 8, 192], BF16) for i in range(2)]
        wk = [sb("wk%d" % i, [128, 4, 128], BF16) for i in range(2)]
        wv = sb("wv", [128, 4, 512], BF16)
        qcn = sb("qcn", [128, 8, TH], BF16)
        kvn = sb("kvn", [128, 4, TH], BF16)
        stage = sb("stage", [128, 8, TH], F32)
        sqt = [sb("sqt%d" % i, [128, TH], BF16) for i in range(2)]
        rbc = sb("rbc", [128, TH], F32)
        rbc2 = sb("rbc2", [128, TH], F32)
        spe = sb("spe", [128, TH], F32)
        kpe_f = sb("kpe_f", [64, TH], F32)
        kpe_rot = sb("kpe_rot", [64, TH], F32)
        ropeT = sb("ropeT", [64, TH], F32)
        ropeU = sb("ropeU", [64, TH], F32)
        qb_f = sb("qb_f", [64, TH], F32)
        cos64 = sb("cos64", [64, TOK], F32)
        sw64 = sb("sw64", [64, TOK], F32)
        par = sb("par", [128, NSP], F32)
        cst_t = sb("cst_t", [128, 2], F32)
        flg_t = sb("flg_t", [128, 2], F32)
        tri = sb("tri", [128, 128], BF16)
        ones = sb("ones", [128, 128], BF16)
        halo = sb("halo", [128, 16, 30], F32)
        hTh = sb("hTh", [128, 32, HALO], BF16)
        rh = sb("rh", [128, HALO], F32)
        sgh = sb("sgh", [128, HALO], F32)
        uh = sb("uh", [128, HALO], F32)
        ubuf = [sb("ubuf%d" % i, [128, 30 + TH], F32) for i in range(2)]
        sig = [sb("sig%d" % i, [128, TH], F32) for i in range(2)]
        acc = [sb("acc%d" % i, [128, TH], F32) for i in range(2)]
        tmpf = [sb("tmpf%d" % i, [128, TH], F32) for i in range(2)]
        mu_bc = sb("mu_bc", [128, TH], F32)
        rs_bc = sb("rs_bc", [128, TH], F32)
        qt_a = [sb("qt_a%d" % i, [128, TH], BF16) for i in range(2)]
        qt_b = [sb("qt_b%d" % i, [64, TH], BF16) for i in range(2)]
        pT = [sb("pT%d" % i, [128, TH], BF16) for i in range(3)]
        ps = [es.enter_context(nc.psum_tensor("ps%d" % i, [128, TH], F32)) for i in range(8)]

        B_hT = [Buf("hT%d" % c) for c in range(32)]
        B_cat = [Buf("cat%d" % c) for c in range(32)]
        B_wbuf = [Buf("wbuf%d" % i) for i in range(NW)]
        B_wq = [Buf("wq%d" % i) for i in range(2)]
        B_wk = [Buf("wk%d" % i) for i in range(2)]
        B_wv = Buf("wv")
        B_qcn = [Buf("qcn%d" % c) for c in range(8)]
        B_kvn = [Buf("kvn%d" % c) for c in range(4)]
        B_stage = [Buf("stage%d" % c) for c in range(8)]
        B_sqt = [Buf("sqt%d" % i) for i in range(2)]
        B = {k: Buf(k) for k in ["rbc", "rbc2", "spe", "kpe_f", "kpe_rot", "ropeT", "ropeU", "qb_f",
                                 "tab", "par", "cst", "mu", "rs", "hTh", "rh", "sgh", "uh",
                                 "xh", "xh_g", "kc_g", "vc_g"]}
        B_halo = [Buf("halo%d" % j) for j in range(16)]
        B_ubuf = [Buf("ubuf%d" % i) for i in range(2)]
        B_sig = [Buf("sig%d" % i) for i in range(2)]
        B_acc = [Buf("acc%d" % i) for i in range(2)]
        B_tmpf = [Buf("tmpf%d" % i) for i in range(2)]
        B_qta = [Buf("qta%d" % i) for i in range(2)]
        B_qtb = [Buf("qtb%d" % i) for i in range(2)]
        B_pT = [Buf("pT%d" % i) for i in range(3)]
        B_ps = [Buf("ps%d" % i) for i in range(8)]
        B_xin = B_stage[0:3]
        xin = [stage[:, i, :] for i in range(3)]
        B_x = [[Buf("x_%d_%d" % (p, j)) for j in range(32)] for p in range(NPL)]
        B_kc = [Buf("kc%d" % h) for h in range(NH)]
        B_vc = [Buf("vc%d" % g) for g in range(8)]
        B_hs = [Buf("hs%d" % p) for p in range(NPL)]
        B_qs = [Buf("qs%d" % p) for p in range(NPL)]

        def flat(ap):
            return ap.rearrange("p a t -> p (a t)")
        vgw = [hT[:, 8 * i:8 * i + 8, :].rearrange("p a (k d) -> p (a k) d", d=256) for i in range(2)]
        Bv_vgw = [B_hT[8 * i:8 * i + 8] for i in range(2)]
        ktw_a = [flat(hT[:, 16 + 8 * i:20 + 8 * i, :]) for i in range(2)]
        ktw_b = [flat(hT[0:64, 20 + 8 * i:24 + 8 * i, :]) for i in range(2)]
        Bv_kta = [B_hT[16 + 8 * i:20 + 8 * i] for i in range(2)]
        Bv_ktb = [B_hT[20 + 8 * i:24 + 8 * i] for i in range(2)]
        vtmp = hT[:, 0:4, :]
        Bv_vtmp = B_hT[0:4]
        ktmp_a = [hT[:, 4 + i, :] for i in range(4)]
        ktmp_b = [hT[0:64, 8 + i, :] for i in range(4)]

        def mm(o, l, r, start, stop, reads, writes, inc=True):
            S.op("pe", lambda e: e.matmul(o, l, r, start=start, stop=stop), reads, writes, inc)

        def act(o, i, func, reads, writes, bias=0.0, scale=1.0):
            S.op("act", lambda e: e.activation(out=o, in_=i, func=func, bias=bias, scale=scale), reads, writes)

        def acopy(o, i, reads, writes):
            S.op("act", lambda e: e.copy(out=o, in_=i), reads, writes)

        def tt(o, a, b, op, reads, writes, E="dve"):
            S.op(E, lambda e: e.tensor_tensor(out=o, in0=a, in1=b, op=op), reads, writes)

        def stt(o, a, s, b, op0, op1, reads, writes, E="dve", nosync=False):
            S.op(E, lambda e: e.scalar_tensor_tensor(out=o, in0=a, scalar=s, in1=b, op0=op0, op1=op1), reads, writes,
                 nosync=nosync)

        def ts(o, a, s1, s2, op0, op1, reads, writes, E="dve"):
            if s2 is None:
                S.op(E, lambda e: e.tensor_scalar(out=o, in0=a, scalar1=s1, scalar2=None, op0=op0), reads, writes)
            else:
                S.op(E, lambda e: e.tensor_scalar(out=o, in0=a, scalar1=s1, scalar2=s2, op0=op0, op1=op1), reads, writes)

        def cp(o, i, reads, writes, E="dve"):
            S.op(E, lambda e: e.tensor_copy(out=o, in_=i), reads, writes)

        def recip(o, i, reads, writes):
            S.op("dve", lambda e: e.reciprocal(out=o, in_=i), reads, writes)

        def rstd_from(dst, dstB, src, srcBs, n):
            act(dst, src, AF.Ln, srcBs, [dstB], bias=EPS, scale=1.0 / n)
            act(dst, dst, AF.Exp, [dstB], [dstB], scale=-0.5)

        S.dma("sp", cst_t[:], cst[:, 0:2], writes=[B["cst"]])
        S.dma("sp", flg_t[:], flg[:, :], writes=[B["cst"]])
        S.dma("pool", tri[:], cst[:, 2:130], writes=[B["cst"]])
        S.op("dve", lambda e: e.memset(ones[:], 1.0), writes=[B["cst"]])
        PI = float(np.pi)
        C1 = 6.28125
        C2 = 2 * PI - C1
        posi = stage[0:64, 0:2, :].bitcast(I32)
        posf = stage[0:64, 2:4, :]
        ang_a = stage[0:64, 4:6, :]
        ang_k = stage[0:64, 6:8, :]
        SB_ = B_stage
        S.dma("sp", posi.rearrange("p a t -> p (a t)"), pos.partition_broadcast(64), writes=SB_)
        cp(posf, posi, SB_, SB_)

        def reduce_sin(dst, shift):
            ts(ang_a, posf, cst_t[0:64, 0:1], shift, ALU.mult, ALU.add, SB_ + [B["cst"]], SB_)
            ts(ang_k, ang_a, 1.0 / (2 * PI), None, ALU.mult, None, SB_, SB_)
            cp(posi, ang_k, SB_, SB_)
            cp(ang_k, posi, SB_, SB_)
            stt(ang_a, ang_k, -C1, ang_a, ALU.mult, ALU.add, SB_, SB_)
            stt(ang_a, ang_k, -C2, ang_a, ALU.mult, ALU.add, SB_, SB_)
            ts(ang_k, ang_a, PI, -2 * PI, ALU.is_gt, ALU.mult, SB_, SB_)
            tt(ang_a, ang_a, ang_k, ALU.add, SB_, SB_)
            ts(ang_k, ang_a, -PI, 2 * PI, ALU.is_lt, ALU.mult, SB_, SB_)
            tt(ang_a, ang_a, ang_k, ALU.add, SB_, SB_)
            act(dst, ang_a, AF.Sin, SB_, [B["tab"]])

        reduce_sin(cos64[:, :].rearrange("p (a t) -> p a t", a=2), PI / 2)
        reduce_sin(sw64[:, :].rearrange("p (a t) -> p a t", a=2), 0.0)
        ts(sw64[:, :], sw64[:, :], cst_t[0:64, 1:2], None, ALU.mult, None, [B["tab"], B["cst"]], [B["tab"]])

        def rope(dst, dstBs, x, xBs, t0):
            cs = cos64[:, t0:t0 + TH]
            sw = sw64[:, t0:t0 + TH]
            tt(ropeT[:, :], x[0:64, :], cs, ALU.mult, xBs + [B["tab"]], [B["ropeT"]])
            tt(ropeU[0:32, :], x[32:64, :], sw[32:64, :], ALU.mult, xBs + [B["tab"]], [B["ropeU"]])
            tt(ropeU[32:64, :], x[0:32, :], sw[0:32, :], ALU.mult, xBs + [B["tab"]], [B["ropeU"]])
            tt(dst, ropeT[:, :], ropeU[:, :], ALU.add, [B["ropeT"], B["ropeU"]], dstBs)

        A1_ORDER = list(range(0, NB_A1))
        A2_ORDER = []
        for jj_ in range(16):
            A2_ORDER += [13 + 2 * jj_, 14 + 2 * jj_, 45 + jj_]
        A2_ORDER += list(range(61, 77))
        wlist = []
        for l in range(NL):
            for p in range(NPL):
                for j in A1_ORDER:
                    wlist.append(("in", l, j))
            for p in range(NPL):
                for j in A2_ORDER:
                    wlist.append(("in", l, j))
                for j in range(32):
                    wlist.append(("out", l, j))
        wstate = {"next": 0, "idx": 0, "main": 0, "sq": 0, "ab": 0, "prev": 0}

        def wprefetch(upto):
            while wstate["next"] <= min(upto, len(wlist) - 1):
                i = wstate["next"]
                kind, l, j = wlist[i]
                if kind == "in":
                    r0 = (l * NB_IN + j) * 128
                    src = w_in_r[r0:r0 + 128, :]
                else:
                    r0 = (l * 32 + j) * 128
                    src = w_out_r[r0:r0 + 128, :]
                S.dma("pool", wbuf[i % NW][:, :, :], src.rearrange("p (a n) -> p a n", a=4),
                      writes=[B_wbuf[i % NW]])
                wstate["next"] += 1

        def wtake(expect):
            i = wstate["idx"]
            assert wlist[i] == expect, (wlist[i], expect)
            wprefetch(i + NW - 1)
            wstate["idx"] += 1
            W = wbuf[i % NW][:, :, :].rearrange("p a (c n) -> p (a c) n", n=128)
            return W, B_wbuf[i % NW]

        def mainbank():
            b = wstate["main"] % 2
            wstate["main"] += 1
            return b

        ABANKS = [0, 1, 6, 7]

        def abank():
            b = ABANKS[wstate["ab"] % 4]
            wstate["ab"] += 1
            return b

        def sqbuf():
            k = wstate["sq"] % 2
            wstate["sq"] += 1
            return k

        O_LNG, O_QA, O_KVA, O_QNA, O_QNB, O_KNA, O_KNB, O_WDW, O_BDW, O_CLG, O_CLB = \
            0, 32, 40, 44, 45, 46, 47, 48, 48 + 496, 48 + 512, 48 + 528
        scale = 1.0 / math.sqrt(192.0)
        pend = []

        def defer_stat(f, delay=1):
            pend.append([delay, f])

        def tick_stat():
            for e in pend:
                e[0] -= 1
            while pend and pend[0][0] <= 0:
                pend.pop(0)[1]()

        def flush_stat():
            while pend:
                pend.pop(0)[1]()

        def phase0(l, lp, xsrc):
            t0 = lp * TH
            for rnd in range(2):
                for g4 in range(8):
                    so = (g4 % 2) * 4
                    st = stage[:, so:so + 4, :]
                    Bst = B_stage[so:so + 4]
                    src = xsrc[g4 * 512:(g4 + 1) * 512, t0:t0 + TH].rearrange("(c q) t -> q c t", q=128)
                    S.dma("sp", st, src, reads=[B_x[lp][g4 * 4 + c] for c in range(4)], writes=Bst)
                    for c8 in range(4):
                        c = g4 * 4 + c8
                        if rnd == 0:
                            k = sqbuf()
                            act(sqt[k][:, :], st[:, c8, :], AF.Square, [Bst[c8]], [B_sqt[k]])
                            mm(ps[2][:, :], ones[:, :], sqt[k][:, :], c == 0, c == 31,
                               [B_sqt[k], B["cst"]], [B_ps[2]])
                        else:
                            stt(hT[:, c, :], st[:, c8, :], par[:, O_LNG + c:O_LNG + c + 1], rbc[:, :],
                                ALU.mult, ALU.mult, [Bst[c8], B["par"], B["rbc"]], [B_hT[c]])
                if rnd == 0:
                    rstd_from(rbc[:, :], B["rbc"], ps[2][:, :], [B_ps[2]], float(D))
            S.dma("sp", hs[lp * D:(lp + 1) * D, :].rearrange("(c q) t -> q c t", q=128), hT[:, :, :],
                  reads=B_hT, writes=[B_hs[lp]])

        def block_mm(W, Bw, bank, M, rhs_of, rhsB_of, inc_last=True, cols=slice(0, TH)):
            for c in range(32):
                mm(ps[bank][0:M, cols], W[:, c, 0:M], rhs_of(c), c == 0, c == 31,
                   [Bw, rhsB_of(c)], [B_ps[bank]], inc=(c == 31 and inc_last) or (c == 31))

        def phaseA(l, lp, jlist):
            t0 = lp * TH
            for bi, j in enumerate(jlist):
                W, Bw = wtake(("in", l, j))
                if pcoll and bi % 11 == 2:
                    pcoll.pop(0)()
                a_bank = wstate["prev"]
                bank = abank()
                wstate["prev"] = bank
                M = 64 if j == 12 else 128
                block_mm(W, Bw, bank, M, lambda c: hT[:, c, :], lambda c: B_hT[c])
                is_conv = 13 <= j < 45
                if is_conv and lp == 0:
                    hb = 4 if (j - 13) % 2 == 0 else 5
                    block_mm(W, Bw, hb, 128, lambda c: hTh[:, c, :], lambda c: B["hTh"], cols=slice(0, HALO))
                tick_stat()
                P = ps[bank]
                BP = B_ps[bank]
                if j < 12:
                    isq = j < 8
                    ci = j if isq else j - 8
                    if isq:
                        stv = [stage[:, c_, :] for c_ in range(8)]
                        Bst = B_stage
                    else:
                        stv = [acc[0][:, :], acc[1][:, :], sig[0][:, :], sig[1][:, :]]
                        Bst = [B_acc[0], B_acc[1], B_sig[0], B_sig[1]]
                    nlast = 7 if isq else 3
                    acopy(stv[ci], P[:, :], [BP], [Bst[ci]])
                    k = sqbuf()
                    act(sqt[k][:, :], P[:, :], AF.Square, [BP], [B_sqt[k]])

                    def _stat(k=k, ci=ci, nlast=nlast):
                        mm(ps[2][:, :], ones[:, :], sqt[k][:, :], ci == 0, ci == nlast,
                           [B_sqt[k], B["cst"]], [B_ps[2]])
                    defer_stat(_stat, 1)
                    if ci == nlast:
                        flush_stat()
                        n = 1024.0 if isq else 512.0
                        rstd_from(rbc2[:, :], B["rbc2"], ps[2][:, :], [B_ps[2]], n)
                        dstT, Bd, og = (qcn, B_qcn, O_QA) if isq else (kvn, B_kvn, O_KVA)
                        for c2 in range(nlast + 1):
                            stt(dstT[:, c2, :], stv[c2], par[:, og + c2:og + c2 + 1], rbc2[:, :],
                                ALU.mult, ALU.mult, [Bst[c2], B["par"], B["rbc2"]], [Bd[c2]])
                        if isq:
                            S.dma("sp", qs[lp * 1024:(lp + 1) * 1024, :].rearrange("(c q) t -> q c t", q=128),
                                  qcn[:, :, :], reads=B_qcn, writes=[B_qs[lp]])
                elif j == 12:
                    acopy(kpe_f[:, :], P[0:64, :], [BP], [B["kpe_f"]])
                    k = sqbuf()
                    act(sqt[k][0:64, :], P[0:64, :], AF.Square, [BP], [B_sqt[k]])
                    mm(ps[2][:, :], ones[0:64, :], sqt[k][0:64, :], True, True, [B_sqt[k], B["cst"]], [B_ps[2]])
                    acopy(spe[:, :], ps[2][:, :], [B_ps[2]], [B["spe"]])
                    ts(kpe_f[:, :], kpe_f[:, :], par[0:64, O_KNB:O_KNB + 1], None, ALU.mult, None,
                       [B["kpe_f"], B["par"]], [B["kpe_f"]])
                    rope(kpe_rot[:, :], [B["kpe_rot"]], kpe_f, [B["kpe_f"]], t0)
                elif j < 45:
                    jj = (j - 13) // 2
                    if (j - 13) % 2 == 1:
                        ub = jj % 2
                        act(sig[ub][:, :], P[:, :], AF.Sigmoid, [BP], [B_sig[ub]])
                        if lp == 0:
                            act(sgh[:, :], ps[5][:, 0:HALO], AF.Sigmoid, [B_ps[5]], [B["sgh"]])
                            tt(uh[:, :], ps[4][:, 0:HALO], sgh[:, :], ALU.mult, [B_ps[4], B["sgh"]], [B["uh"]])
                            ts(ubuf[ub][:, 0:30], uh[:, 2:HALO], flg_t[:, 1:2], None, ALU.mult, None,
                               [B["uh"], B["cst"]], [B_ubuf[ub]])
                        else:
                            cp(ubuf[ub][:, 0:30], halo[:, jj, :], [B_halo[jj]], [B_ubuf[ub]])
                        tt(ubuf[ub][:, 30:30 + TH], ps[a_bank][:, :], sig[ub][:, :], ALU.mult,
                           [B_ps[a_bank], B_sig[ub]], [B_ubuf[ub]])
                        if lp < NPL - 1:
                            cp(halo[:, jj, :], ubuf[ub][:, TH:TH + 30], [B_ubuf[ub]], [B_halo[jj]])
                        wo = O_WDW + jj * 31
                        ts(acc[ub][:, :], ubuf[ub][:, 0:TH], par[:, wo:wo + 1],
                           par[:, O_BDW + jj:O_BDW + jj + 1], ALU.mult, ALU.add,
                           [B_ubuf[ub], B["par"]], [B_acc[ub]])
                        for tap in range(1, 31):
                            last = tap == 30
                            o_ap = catT[:, 16 + jj, :] if last else acc[ub][:, :]
                            stt(o_ap, ubuf[ub][:, tap:tap + TH], par[:, wo + tap:wo + tap + 1], acc[ub][:, :],
                                ALU.mult, ALU.add, [B_ubuf[ub], B["par"], B_acc[ub]],
                                [B_cat[16 + jj]] if last else [B_acc[ub]], nosync=(tap > 1))
                        kk = {}

                        def _sq(jj=jj, kk=kk):
                            kk["k"] = sqbuf()
                            act(sqt[kk["k"]][:, :], catT[:, 16 + jj, :], AF.Square, [B_cat[16 + jj]], [B_sqt[kk["k"]]])

                        def _stat(jj=jj, kk=kk):
                            k = kk["k"]
                            mm(ps[2][:, :], ones[:, :], catT[:, 16 + jj, :], jj == 0, jj == 15,
                               [B_cat[16 + jj], B["cst"]], [B_ps[2]])
                            mm(ps[3][:, :], ones[:, :], sqt[k][:, :], jj == 0, jj == 15,
                               [B_sqt[k], B["cst"]], [B_ps[3]])
                        defer_stat(_sq, 3)
                        defer_stat(_stat, 4)
                        if jj == 15:
                            flush_stat()
                            ts(mu_bc[:, :], ps[2][:, :], 1.0 / 2048, None, ALU.mult, None, [B_ps[2]], [B["mu"]])
                            tt(rs_bc[:, :], mu_bc[:, :], mu_bc[:, :], ALU.mult, [B["mu"]], [B["rs"]])
                            stt(rs_bc[:, :], ps[3][:, :], 1.0 / 2048, rs_bc[:, :], ALU.mult, ALU.subtract,
                                [B_ps[3], B["rs"]], [B["rs"]])
                            rstd_from(rs_bc[:, :], B["rs"], rs_bc[:, :], [B["rs"]], 1.0)
                elif j < 61:
                    h = j - 45
                    act(catT[:, h, :], P[:, :], AF.Silu, [BP], [B_cat[h]])
                else:
                    jj = j - 61
                    tb = jj % 2
                    act(sig[tb][:, :], P[:, :], AF.Silu, [BP], [B_sig[tb]])
                    tt(tmpf[tb][:, :], catT[:, 16 + jj, :], mu_bc[:, :], ALU.subtract,
                       [B_cat[16 + jj], B["mu"]], [B_tmpf[tb]])
                    tt(tmpf[tb][:, :], tmpf[tb][:, :], rs_bc[:, :], ALU.mult, [B_tmpf[tb], B["rs"]], [B_tmpf[tb]])
                    act(tmpf[tb][:, :], tmpf[tb][:, :], AF.Silu, [B_tmpf[tb], B["par"]], [B_tmpf[tb]],
                        bias=par[:, O_CLB + jj:O_CLB + jj + 1], scale=par[:, O_CLG + jj:O_CLG + jj + 1])
                    tt(catT[:, 16 + jj, :], tmpf[tb][:, :], sig[tb][:, :], ALU.mult,
                       [B_tmpf[tb], B_sig[tb]], [B_cat[16 + jj]])
            flush_stat()

        def kv_stage(l, lp):
            t0 = lp * TH
            rb = [rbc2, rbc]
            Brb = [B["rbc2"], B["rbc"]]
            sbank = [3, 2]

            def load_wk(h):
                r0 = (l * NH + h) * 128
                S.dma("pool", wk[h % 2][:, :, :], wk_r[r0:r0 + 128, :].rearrange("p (c n) -> p c n", c=4),
                      writes=[B_wk[h % 2]])

            def v_group(g):
                r0 = (l * 4 + g) * 128
                S.dma("pool", wv[:, :, :], wv_r[r0:r0 + 128, :].rearrange("p (c n) -> p c n", c=4),
                      writes=[B_wv])
                for tt_ in range(4):
                    bank = 6 + tt_ % 2
                    for c in range(4):
                        mm(ps[bank][:, :], kvn[:, c, tt_ * 128:(tt_ + 1) * 128], wv[:, c, :], c == 0, c == 3,
                           [B_kvn[c], B_wv], [B_ps[bank]], inc=(c == 3))
                    acopy(vtmp[:, tt_, :], ps[bank][:, :], [B_ps[bank]], [Bv_vtmp[tt_]])
                for k2 in range(2):
                    pr = 2 * g + k2
                    S.dma("sp", vcp_rows(pr, t0, t0 + TH).rearrange("(t q) d -> q t d", q=128),
                          vtmp[:, :, k2 * 256:(k2 + 1) * 256], reads=Bv_vtmp, writes=[B_vc[pr]])

            KB = [0, 1, 4, 5]

            def k_mm(h):
                WK = wk[h % 2]
                bk = KB[h % 4]
                for c in range(4):
                    mm(ps[bk][:, :], WK[:, c, :], kvn[:, c, :], c == 0, c == 3,
                       [B_wk[h % 2], B_kvn[c]], [B_ps[bk]], inc=(c == 3))
                k = sqbuf()
                act(sqt[k][:, :], ps[bk][:, :], AF.Square, [B_ps[bk]], [B_sqt[k]])
                return bk, k

            def k_chain1(h, bk, k):
                i2 = h % 2
                sbk = sbank[i2]
                mm(ps[sbk][:, :], ones[:, :], sqt[k][:, :], True, True, [B_sqt[k], B["cst"]], [B_ps[sbk]])
                tt(rb[i2][:, :], ps[sbk][:, :], spe[:, :], ALU.add, [B_ps[sbk], B["spe"]], [Brb[i2]])
                rstd_from(rb[i2][:, :], Brb[i2], rb[i2][:, :], [Brb[i2]], 192.0)

            def k_chain2(h, bk, k):
                i2 = h % 2
                i4 = h % 4
                stt(ktmp_a[i4], ps[bk][:, :], par[:, O_KNA:O_KNA + 1], rb[i2][:, :], ALU.mult, ALU.mult,
                    [B_ps[bk], B["par"], Brb[i2]], [B_hT[4 + i4]])
                tt(ktmp_b[i4], kpe_rot[:, :], rb[i2][0:64, :], ALU.mult,
                   [B["kpe_rot"], Brb[i2]], [B_hT[8 + i4]])
                S.dma("sp", kc_a_rows(h)[:, t0:t0 + TH], ktmp_a[i4], reads=[B_hT[4 + i4]], writes=[B_kc[h]])
                S.dma("sp", kc_b[h * 64:(h + 1) * 64, t0:t0 + TH], ktmp_b[i4], reads=[B_hT[8 + i4]], writes=[B_kc[h]])

            load_wk(0)
            load_wk(1)
            cur = k_mm(0)
            k_chain1(0, *cur)
            for h in range(NH):
                if h % 4 == 0:
                    v_group(h // 4)
                nxt = k_mm(h + 1) if h + 1 < NH else None
                if nxt is not None:
                    k_chain1(h + 1, *nxt)
                k_chain2(h, *cur)
                if h + 2 < NH:
                    load_wk(h + 2)
                cur = nxt

        pcoll = []

        def exchange(l):
            S.coll(xh_t.ap().opt(), xh_gt.ap().opt(), groups, reads=[B["xh"]], writes=[B["xh_g"]])
            for i in range(2):
                pcoll.append(lambda i=i: S.coll(kc_a_t[i].ap().opt(), kc_a_gt[i].ap().opt(), groups,
                                                reads=B_kc, writes=[B["kc_g"]]))
            pcoll.append(lambda: S.coll(kc_b_t.ap().opt(), kc_b_gt.ap().opt(), groups, reads=B_kc, writes=[B["kc_g"]]))
            for i in range(2):
                pcoll.append(lambda i=i: S.coll(vcp_t[i].ap().opt(), vcp_gt[i].ap().opt(), groups,
                                                reads=B_vc, writes=[B["vc_g"]]))

        def flush_coll():
            while pcoll:
                pcoll.pop(0)()

        def halo_prep(l):
            xhs = stage[:, 0:2, :].rearrange("p a (c t) -> p (a c) t", t=HALO)
            Bst = B_stage[0:2]
            S.dma("sp", xhs, xh_g[0:D, :].rearrange("(c q) t -> q c t", q=128), reads=[B["xh_g"]], writes=Bst)
            sqh = sqt[0][:, :].rearrange("p (c t) -> p c t", t=HALO)
            for half in range(2):
                act(sqh, xhs[:, half * 16:(half + 1) * 16, :], AF.Square, Bst, [B_sqt[0]])
                for c in range(16):
                    cc = half * 16 + c
                    mm(ps[2][:, 0:HALO], ones[:, :], sqh[:, c, :], cc == 0, cc == 31, [B_sqt[0], B["cst"]], [B_ps[2]])
            rstd_from(rh[:, :], B["rh"], ps[2][:, 0:HALO], [B_ps[2]], float(D))
            for c in range(32):
                stt(hTh[:, c, :], xhs[:, c, :], par[:, O_LNG + c:O_LNG + c + 1], rh[:, :],
                    ALU.mult, ALU.mult, Bst + [B["par"], B["rh"]], [B["hTh"]])

        def attention(l, lp):
            t0 = lp * TH
            nown = (t0 // 128) + 4
            nkt = 8 + nown

            def load_wq(h):
                r0 = (l * NH + h) * 128
                S.dma("pool", wq[h % 2][:, :, :], wq_r[r0:r0 + 128, :].rearrange("p (c n) -> p c n", c=8),
                      writes=[B_wq[h % 2]])

            def prepA(h):
                i2 = h % 2
                S.dma("sp", ktw_a[i2][:, 0:TOK], kc_a_g_rows(h), reads=[B["kc_g"]], writes=Bv_kta[i2])
                S.dma("sp", ktw_a[i2][:, TOK:TOK + t0 + TH], kc_a_rows(h)[:, 0:t0 + TH],
                      reads=[B_kc[h]], writes=Bv_kta[i2])
                S.dma("sp", ktw_b[i2][:, 0:TOK], kc_b_g[h * 64:(h + 1) * 64, :], reads=[B["kc_g"]], writes=Bv_ktb[i2])
                S.dma("sp", ktw_b[i2][:, TOK:TOK + t0 + TH], kc_b[h * 64:(h + 1) * 64, 0:t0 + TH],
                      reads=[B_kc[h]], writes=Bv_ktb[i2])
                if h % 2 == 0:
                    pr = h // 2
                    vb = pr % 2
                    S.dma("sp", vgw[vb][:, 0:8, :], vcp_g_rows(pr).rearrange("(t q) d -> q t d", q=128),
                          reads=[B["vc_g"]], writes=Bv_vgw[vb])
                    S.dma("sp", vgw[vb][:, 8:8 + nown, :],
                          vcp_rows(pr, 0, t0 + TH).rearrange("(t q) d -> q t d", q=128),
                          reads=[B_vc[pr]], writes=Bv_vgw[vb])
                WQ = wq[i2]
                bqa = mainbank()
                for c in range(8):
                    mm(ps[bqa][:, :], WQ[:, c, 0:128], qcn[:, c, :], c == 0, c == 7,
                       [B_wq[i2], B_qcn[c]], [B_ps[bqa]], inc=(c == 7))
                bqb = mainbank()
                for c in range(8):
                    mm(ps[bqb][0:64, :], WQ[:, c, 128:192], qcn[:, c, :], c == 0, c == 7,
                       [B_wq[i2], B_qcn[c]], [B_ps[bqb]], inc=(c == 7))
                act(sqt[0][:, :], ps[bqa][:, :], AF.Square, [B_ps[bqa]], [B_sqt[0]])
                act(sqt[1][0:64, :], ps[bqb][0:64, :], AF.Square, [B_ps[bqb]], [B_sqt[1]])
                return bqa, bqb

            def prepB(h, bqa, bqb):
                i2 = h % 2
                mm(ps[2][:, :], ones[:, :], sqt[0][:, :], True, False, [B_sqt[0], B["cst"]], [B_ps[2]], inc=False)
                mm(ps[2][:, :], ones[0:64, :], sqt[1][0:64, :], False, True, [B_sqt[1], B["cst"]], [B_ps[2]])
                rstd_from(rbc[:, :], B["rbc"], ps[2][:, :], [B_ps[2]], 192.0)
                stt(qt_a[i2][:, :], ps[bqa][:, :], par[:, O_QNA:O_QNA + 1], rbc[:, :], ALU.mult, ALU.mult,
                    [B_ps[bqa], B["par"], B["rbc"]], [B_qta[i2]])
                stt(qb_f[:, :], ps[bqb][0:64, :], par[0:64, O_QNB:O_QNB + 1], rbc[0:64, :], ALU.mult, ALU.mult,
                    [B_ps[bqb], B["par"], B["rbc"]], [B["qb_f"]])
                rope(qt_b[i2][:, :], [B_qtb[i2]], qb_f, [B["qb_f"]], t0)

            def tile_geo(kt):
                jd = kt - 8 - t0 // 128
                q0 = jd * 128 if jd > 0 else 0
                return jd, q0, (4, 5, 3)[kt % 3], kt % 3

            def score_mm(h, kt):
                i2 = h % 2
                jd, q0, sb_, pi_ = tile_geo(kt)
                mm(ps[sb_][:, q0:TH], ktw_a[i2][:, kt * 128:(kt + 1) * 128], qt_a[i2][:, q0:TH], True, False,
                   Bv_kta[i2] + [B_qta[i2]], [B_ps[sb_]], inc=False)
                mm(ps[sb_][:, q0:TH], ktw_b[i2][:, kt * 128:(kt + 1) * 128], qt_b[i2][:, q0:TH], False, True,
                   Bv_ktb[i2] + [B_qtb[i2]], [B_ps[sb_]])

            def score_rest(h, kt):
                i2 = h % 2
                vb = (h // 2) % 2
                vs = (h % 2) * 128
                own = kt >= 8
                jd, q0, sb_, pi_ = tile_geo(kt)
                bias = -8.0 if own else flg_t[:, 0:1]
                act(pT[pi_][:, q0:TH], ps[sb_][:, q0:TH], AF.Exp, [B_ps[sb_], B["cst"]], [B_pT[pi_]],
                    bias=bias, scale=scale)
                if jd >= 0:
                    tt(pT[pi_][:, q0:q0 + 128], pT[pi_][:, q0:q0 + 128], tri[:, :], ALU.mult,
                       [B_pT[pi_], B["cst"]], [B_pT[pi_]])
                mm(ps[6][:, q0:TH], vgw[vb][:, kt, vs:vs + 128], pT[pi_][:, q0:TH],
                   kt == 0, kt == nkt - 1, Bv_vgw[vb] + [B_pT[pi_]], [B_ps[6]], inc=(kt == nkt - 1))
                mm(ps[7][:, q0:TH], ones[:, :], pT[pi_][:, q0:TH],
                   kt == 0, kt == nkt - 1, [B["cst"], B_pT[pi_]], [B_ps[7]])

            def finish(h):
                tb = h % 2
                act(tmpf[tb][:, :], ps[7][:, :], AF.Ln, [B_ps[7]], [B_tmpf[tb]])
                act(tmpf[tb][:, :], tmpf[tb][:, :], AF.Exp, [B_tmpf[tb]], [B_tmpf[tb]], scale=-1.0)
                tt(tmpf[tb][:, :], ps[6][:, :], tmpf[tb][:, :], ALU.mult, [B_ps[6], B_tmpf[tb]], [B_tmpf[tb]])
                tt(catT[:, h, :], tmpf[tb][:, :], catT[:, h, :], ALU.mult, [B_tmpf[tb], B_cat[h]], [B_cat[h]])

            load_wq(0)
            load_wq(1)
            ba, bb = prepA(0)
            prepB(0, ba, bb)
            for h in range(NH):
                nxt = None
                if h + 1 < NH:
                    nxt = prepA(h + 1)
                score_mm(h, 0)
                score_mm(h, 1)
                for kt in range(nkt):
                    if kt + 2 < nkt:
                        score_mm(h, kt + 2)
                    score_rest(h, kt)
                    if kt == 2 and nxt is not None:
                        prepB(h + 1, *nxt)
                        if h + 2 < NH:
                            load_wq(h + 2)
                finish(h)

        def phaseC(l, lp, xsrc):
            t0 = lp * TH
            for jo in range(32):
                W, Bw = wtake(("out", l, jo))
                bank = abank()
                xi = jo % 3
                S.dma("sp", xin[xi], xsrc[jo * 128:(jo + 1) * 128, t0:t0 + TH],
                      reads=[B_x[lp][jo]], writes=[B_xin[xi]])
                block_mm(W, Bw, bank, 128, lambda c: catT[:, c, :], lambda c: B_cat[c])
                tt(xin[xi], ps[bank][:, :], xin[xi], ALU.add, [B_ps[bank], B_xin[xi]], [B_xin[xi]])
                S.dma("sp", out[jo * 128:(jo + 1) * 128, t0:t0 + TH], xin[xi],
                      reads=[B_xin[xi]], writes=[B_x[lp][jo]])

        wprefetch(NW - 1)
        for l in range(NL):
            xsrc = xT if l == 0 else out
            S.dma("sp", par[:, :], sp_r[l * 128:(l + 1) * 128, :], writes=[B["par"]])
            for lp in range(NPL):
                phase0(l, lp, xsrc)
                if lp == NPL - 1:
                    for q4 in range(4):
                        S.dma("sp", xh[q4 * 1024:(q4 + 1) * 1024, :], xsrc[q4 * 1024:(q4 + 1) * 1024, TOK - HALO:TOK],
                              reads=B_x[lp][q4 * 8:(q4 + 1) * 8], writes=[B["xh"]])
                phaseA(l, lp, A1_ORDER)
                kv_stage(l, lp)
            exchange(l)
            for lp in range(NPL):
                S.dma("sp", hT[:, :, :], hs[lp * D:(lp + 1) * D, :].rearrange("(c q) t -> q c t", q=128),
                      reads=[B_hs[lp]], writes=B_hT)
                S.dma("sp", qcn[:, :, :], qs[lp * 1024:(lp + 1) * 1024, :].rearrange("(c q) t -> q c t", q=128),
                      reads=[B_qs[lp]], writes=B_qcn)
                if lp == 0:
                    halo_prep(l)
                phaseA(l, lp, A2_ORDER)
                flush_coll()
                attention(l, lp)
                phaseC(l, lp, xsrc)
        for p in range(NPL):
            S.wait_bufs("sp", B_x[p])
        print("program built: insts=%d, sems pe=%d act=%d dve=%d" % (S.n_inst, S.cnt["pe"], S.cnt["act"], S.cnt["dve"]))
    return nc


def _in_col_order():
    blocks = []
    blocks += [np.arange(i * 128, (i + 1) * 128) for i in range(8)]
    blocks += [np.arange(1024 + i * 128, 1024 + (i + 1) * 128) for i in range(4)]
    kpe = np.full(128, -1)
    kpe[:64] = np.arange(1536, 1600)
    blocks.append(kpe)
    for j in range(16):
        blocks.append(np.arange(3648 + j * 128, 3648 + (j + 1) * 128))
        blocks.append(np.arange(3648 + 2048 + j * 128, 3648 + 2048 + (j + 1) * 128))
    blocks += [np.arange(1600 + h * 128, 1600 + (h + 1) * 128) for h in range(16)]
    blocks += [np.arange(7744 + j * 128, 7744 + (j + 1) * 128) for j in range(16)]
    return blocks


def _block_layout(w, blocks):
    K = w.shape[0]
    kc = K // 128
    outp = np.zeros((len(blocks), 128, kc, 128), np.float32)
    w3 = w.reshape(kc, 128, w.shape[1])
    for j, cols in enumerate(blocks):
        valid = cols >= 0
        sub = w3[:, :, cols[valid]]
        outp[j, :, :, :valid.sum()] = sub.transpose(1, 0, 2)
    return outp.reshape(len(blocks) * 128, kc * 128)


def prepare_weights(NL, ln_g, w_in, q_a_norm, w_q_up, kv_a_norm, w_kv_up, q_norm, k_norm,
                    w_dw, b_dw, conv_ln_g, conv_ln_b, w_out):
    blocks_in = _in_col_order()
    blocks_out = [np.arange(j * 128, (j + 1) * 128) for j in range(32)]
    w_in_r = np.concatenate([_block_layout(np.asarray(w_in[l]), blocks_in) for l in range(NL)], 0)
    w_out_r = np.concatenate([_block_layout(np.asarray(w_out[l]), blocks_out) for l in range(NL)], 0)
    wq_l, wk_l, wv_l, sp_l = [], [], [], []
    for l in range(NL):
        wq = np.asarray(w_q_up[l]).reshape(8, 128, NH, 192)
        wq_l.append(wq.transpose(2, 1, 0, 3).reshape(NH * 128, 8 * 192))
        wkv = np.asarray(w_kv_up[l]).reshape(4, 128, NH, 256)
        wk_l.append(wkv[:, :, :, :128].transpose(2, 1, 0, 3).reshape(NH * 128, 4 * 128))
        wv = wkv[:, :, :, 128:].reshape(4, 128, 4, 4, 128)
        wv_l.append(wv.transpose(2, 1, 0, 3, 4).reshape(4 * 128, 4 * 512))
        sp = np.zeros((128, NSP), np.float32)
        sp[:, 0:32] = np.asarray(ln_g[l]).reshape(32, 128).T
        sp[:, 32:40] = np.asarray(q_a_norm[l]).reshape(8, 128).T
        sp[:, 40:44] = np.asarray(kv_a_norm[l]).reshape(4, 128).T
        sp[:, 44] = np.asarray(q_norm[l])[:128]
        sp[:64, 45] = np.asarray(q_norm[l])[128:]
        sp[:, 46] = np.asarray(k_norm[l])[:128]
        sp[:64, 47] = np.asarray(k_norm[l])[128:]
        sp[:, 48:48 + 496] = np.asarray(w_dw[l]).reshape(31, 16, 128).transpose(2, 1, 0).reshape(128, 496)
        sp[:, 544:560] = np.asarray(b_dw[l]).reshape(16, 128).T
        sp[:, 560:576] = np.asarray(conv_ln_g[l]).reshape(16, 128).T
        sp[:, 576:592] = np.asarray(conv_ln_b[l]).reshape(16, 128).T
        sp_l.append(sp)
    cst = np.zeros((128, 130), np.float32)
    half = 32
    inv_freq = (10000.0 ** (-np.arange(half, dtype=np.float32) / half)).astype(np.float32)
    cst[:64, 0] = np.concatenate([inv_freq, inv_freq])
    cst[:32, 1] = 1.0
    cst[32:64, 1] = -1.0
    kk = np.arange(128)
    cst[:, 2:130] = (kk[None, :] >= kk[:, None]).astype(np.float32)
    return dict(
        w_in_r=np.ascontiguousarray(w_in_r), w_out_r=np.ascontiguousarray(w_out_r),
        wq_r=np.ascontiguousarray(np.concatenate(wq_l, 0)), wk_r=np.ascontiguousarray(np.concatenate(wk_l, 0)),
        wv_r=np.ascontiguousarray(np.concatenate(wv_l, 0)), sp_r=np.ascontiguousarray(np.concatenate(sp_l, 0)),
        cst=cst)


def run(NL, n_cores, x, positions, wd):
    nc = build_program(NL, n_cores)
    in_maps = []
    for c in range(n_cores):
        b, half = c // 2, c % 2
        m = dict(wd)
        m["xT"] = np.ascontiguousarray(np.asarray(x[b])[half * TOK:(half + 1) * TOK].T)
        m["pos"] = np.ascontiguousarray(np.asarray(positions[b])[half * TOK:(half + 1) * TOK].reshape(1, TOK).astype(np.int32))
        f = np.zeros((128, 2), np.float32)
        f[:, 0] = -8.0 if half == 1 else -30000.0
        f[:, 1] = 1.0 if half == 1 else 0.0
        m["flg"] = f
        in_maps.append(m)
    res = run_bass_kernel_spmd(nc, in_maps, core_ids=list(range(n_cores)))
    y = np.zeros((n_cores // 2, SEQ, D), np.float32)
    for c in range(n_cores):
        b, half = c // 2, c % 2
        y[b, half * TOK:(half + 1) * TOK] = res.results[c]["out"].T
    return y


def kernel(x, positions, ln_g, w_in, q_a_norm, w_q_up, kv_a_norm, w_kv_up, q_norm, k_norm,
           w_dw, b_dw, conv_ln_g, conv_ln_b, w_out):
    wd = prepare_weights(DEPTH, ln_g, w_in, q_a_norm, w_q_up, kv_a_norm, w_kv_up, q_norm, k_norm,
                         w_dw, b_dw, conv_ln_g, conv_ln_b, w_out)
    y = run(DEPTH, N_CORES, np.asarray(x), np.asarray(positions), wd)
    return y.astype(np.float32)
```
